# Optimizing a Trainium2 kernel written in Bass

```python
import jax, jax.numpy as jnp
from jax import lax
import numpy as np

D_MODEL = 1024
BATCH = 32
SEQ = 2048
DEPTH = 1

MLA_HEADS = 8
MLA_NOPE = 64
MLA_ROPE = 32
MLA_V = 64
Q_LORA = 384
KV_LORA = 256
MLA_WIDTH = MLA_HEADS * MLA_V

SWA_HEADS = 8
SWA_KV_HEADS = 2
SWA_HEAD_DIM = 64
SWA_GROUP = SWA_HEADS // SWA_KV_HEADS
SWA_WIDTH = SWA_HEADS * SWA_HEAD_DIM
SWA_KV_WIDTH = SWA_KV_HEADS * SWA_HEAD_DIM
WINDOW = 128

D_MIX = MLA_WIDTH + SWA_WIDTH
Q_BLOCK = 128
ROPE_THETA = 10000.0
EPS = 1e-6
ALIBI_MAX_EXP = 8.0

IN_SPLITS = (Q_LORA, KV_LORA, MLA_ROPE, MLA_WIDTH,
             SWA_WIDTH, SWA_KV_WIDTH, SWA_KV_WIDTH, SWA_WIDTH)
D_IN = int(sum(IN_SPLITS))
SPLIT_IDX = [int(v) for v in np.cumsum(IN_SPLITS)[:-1]]

kernel_name = 'hymba_mla_swa_adaln_block'


def rmsnorm(t, gain):
    tf = t.astype(jnp.float32)
    y = tf * lax.rsqrt(jnp.mean(tf * tf, axis=-1, keepdims=True) + EPS)
    return (y * gain.astype(jnp.float32)).astype(t.dtype)


def rope_cos_sin(positions):
    inv = ROPE_THETA ** (-jnp.arange(0, MLA_ROPE, 2, dtype=jnp.float32) / MLA_ROPE)
    ang = positions.astype(jnp.float32)[..., None] * inv
    return jnp.cos(ang), jnp.sin(ang)


def apply_rope(t, cos, sin):
    tf = t.astype(jnp.float32)
    t1, t2 = jnp.split(tf, 2, axis=-1)
    return jnp.concatenate([t1 * cos - t2 * sin, t2 * cos + t1 * sin], axis=-1).astype(t.dtype)


def alibi_slopes(n_heads):
    h = jnp.arange(1, n_heads + 1, dtype=jnp.float32)
    return 2.0 ** (-ALIBI_MAX_EXP * h / n_heads)


def mla_attention(q_nope, q_pe, k_nope, k_pe, v):
    B, S = q_nope.shape[0], q_nope.shape[1]
    nb = S // Q_BLOCK
    scale = (MLA_NOPE + MLA_ROPE) ** -0.5
    key_idx = jnp.arange(S)

    def to_blocks(t):
        return jnp.moveaxis(t.reshape(B, nb, Q_BLOCK, *t.shape[2:]), 1, 0)

    def one_block(args):
        qn, qp, blk = args
        s = (jnp.einsum('bqhd,bshd->bhqs', qn, k_nope, preferred_element_type=jnp.float32)
             + jnp.einsum('bqhr,bsr->bhqs', qp, k_pe, preferred_element_type=jnp.float32)) * scale
        q_idx = blk * Q_BLOCK + jnp.arange(Q_BLOCK)
        s = jnp.where(key_idx[None, :] <= q_idx[:, None], s, -jnp.inf)
        p = jax.nn.softmax(s, axis=-1).astype(v.dtype)
        return jnp.einsum('bhqs,bshd->bqhd', p, v)

    o = lax.map(one_block, (to_blocks(q_nope), to_blocks(q_pe), jnp.arange(nb)))
    return jnp.moveaxis(o, 0, 1).reshape(B, S, MLA_WIDTH)


def swa_attention(q, k, v, positions, slopes, sinks):
    B, S = q.shape[0], q.shape[1]
    nb = S // WINDOW
    scale = SWA_HEAD_DIM ** -0.5

    def band(t):
        tb = t.reshape(B, nb, WINDOW, *t.shape[2:])
        prev = jnp.concatenate([jnp.zeros_like(tb[:, :1]), tb[:, :-1]], axis=1)
        return jnp.concatenate([prev, tb], axis=2)

    qb = q.reshape(B, nb, WINDOW, SWA_KV_HEADS, SWA_GROUP, SWA_HEAD_DIM)
    kb = band(k.reshape(B, S, SWA_KV_HEADS, SWA_HEAD_DIM))
    vb = band(v.reshape(B, S, SWA_KV_HEADS, SWA_HEAD_DIM))
    s = jnp.einsum('bnqkgd,bnskd->bnkgqs', qb, kb, preferred_element_type=jnp.float32) * scale
    pq = positions.reshape(B, nb, WINDOW)
    pk = band(positions)
    dist = (pq[..., :, None] - pk[..., None, :]).astype(jnp.float32)
    s = s - slopes.reshape(SWA_KV_HEADS, SWA_GROUP)[None, None, :, :, None, None] * dist[:, :, None, None]
    i = jnp.arange(WINDOW)[:, None]
    j = jnp.arange(2 * WINDOW)[None, :]
    rel = WINDOW + i - j
    blk = jnp.arange(nb)[:, None, None]
    valid = (rel >= 0) & (rel < WINDOW) & ((blk > 0) | (j >= WINDOW))
    s = jnp.where(valid[None, :, None, None], s, -jnp.inf)
    sink = jnp.broadcast_to(sinks.astype(jnp.float32).reshape(SWA_KV_HEADS, SWA_GROUP)[None, None, :, :, None, None],
                            s.shape[:-1] + (1,))
    p = jax.nn.softmax(jnp.concatenate([s, sink], axis=-1), axis=-1)[..., :-1].astype(v.dtype)
    o = jnp.einsum('bnkgqs,bnskd->bnqkgd', p, vb)
    return o.reshape(B, S, SWA_WIDTH)


def setup_inputs(seed: int = 0) -> dict:
    key = jax.random.key(seed)
    ks = jax.random.split(key, 16)
    f32 = jnp.float32
    nrm = lambda k, shape, s: jax.random.normal(k, shape, f32) * s
    x = jax.random.normal(ks[0], (BATCH, SEQ, D_MODEL), f32)
    c = jax.random.normal(ks[1], (BATCH, D_MODEL), f32)
    offs = jax.random.randint(ks[2], (BATCH, 1), 0, 1024, dtype=jnp.int32)
    positions = offs + jnp.arange(SEQ, dtype=jnp.int32)[None, :]
    return {
        'x': x,
        'c': c,
        'positions': positions,
        'w_ada': nrm(ks[3], (DEPTH, D_MODEL, 3 * D_MODEL), D_MODEL ** -0.5),
        'b_ada': nrm(ks[4], (DEPTH, 3 * D_MODEL), 0.02),
        'norm_gain': 1.0 + nrm(ks[5], (DEPTH, D_MODEL), 0.02),
        'w_in': nrm(ks[6], (DEPTH, D_MODEL, D_IN), D_MODEL ** -0.5),
        'q_norm_gain': 1.0 + nrm(ks[7], (DEPTH, Q_LORA), 0.02),
        'kv_norm_gain': 1.0 + nrm(ks[8], (DEPTH, KV_LORA), 0.02),
        'w_uq': nrm(ks[9], (DEPTH, Q_LORA, MLA_HEADS * (MLA_NOPE + MLA_ROPE)), Q_LORA ** -0.5),
        'w_ukv': nrm(ks[10], (DEPTH, KV_LORA, MLA_HEADS * (MLA_NOPE + MLA_V)), KV_LORA ** -0.5),
        'swa_sinks': nrm(ks[11], (DEPTH, SWA_HEADS), 1.0),
        'w_out': nrm(ks[12], (DEPTH, D_MIX, D_MODEL), D_MIX ** -0.5),
        'final_gain': 1.0 + nrm(ks[13], (D_MODEL,), 0.02),
    }


def reference(x, c, positions, w_ada, b_ada, norm_gain, w_in, q_norm_gain, kv_norm_gain,
              w_uq, w_ukv, swa_sinks, w_out, final_gain):
    B, S, _ = x.shape
    cos, sin = rope_cos_sin(positions)
    slopes = alibi_slopes(SWA_HEADS)
    c_act = jax.nn.silu(c)
    for l in range(DEPTH):
        mod = c_act @ w_ada[l] + b_ada[l]
        shift, scale, gate = jnp.split(mod, 3, axis=-1)
        h = rmsnorm(x, norm_gain[l]) * (1.0 + scale[:, None, :]) + shift[:, None, :]
        z = h @ w_in[l]
        zq, zkv, kr, g_mla, q_s, k_s, v_s, g_swa = jnp.split(z, SPLIT_IDX, axis=-1)
        q = (rmsnorm(zq, q_norm_gain[l]) @ w_uq[l]).reshape(B, S, MLA_HEADS, MLA_NOPE + MLA_ROPE)
        q_nope, q_pe = q[..., :MLA_NOPE], apply_rope(q[..., MLA_NOPE:], cos[:, :, None, :], sin[:, :, None, :])
        kv = (rmsnorm(zkv, kv_norm_gain[l]) @ w_ukv[l]).reshape(B, S, MLA_HEADS, MLA_NOPE + MLA_V)
        k_nope, v_mla = kv[..., :MLA_NOPE], kv[..., MLA_NOPE:]
        k_pe = apply_rope(kr, cos, sin)
        o_mla = mla_attention(q_nope, q_pe, k_nope, k_pe, v_mla).astype(x.dtype)
        o_swa = swa_attention(q_s, k_s, v_s, positions, slopes, swa_sinks[l]).astype(x.dtype)
        y = jnp.concatenate([o_mla * jax.nn.silu(g_mla), o_swa * jax.nn.silu(g_swa)], axis=-1) @ w_out[l]
        x = x + gate[:, None, :] * y
    return rmsnorm(x, final_gain)
```

```python
import numpy as np
from contextlib import ExitStack
import concourse.bass as bass
import concourse.mybir as mybir
from concourse.bass_utils import run_bass_kernel_spmd

F32 = mybir.dt.float32
BF16 = mybir.dt.bfloat16
I32 = mybir.dt.int32
ALU = mybir.AluOpType
AF = mybir.ActivationFunctionType

NCORES = 8
NB = 4
S = 2048
D = 1024
CH = 512
NCH = S // CH
EPS = 1e-6
NEG = -30000.0
SC_MLA = float(96 ** -0.5)
SC_SWA = 0.125
TWO_PI = float(2.0 * np.pi)

C_ZQ, C_ZKV, C_K1, C_K2, C_GM, C_QS, C_VS, C_GS = 0, 384, 640, 768, 832, 1344, 1856, 1984
NWIN = 2496
SLOT_OF_G = {0: 0, 2: 1, 1: 2, 3: 3}


class Op:
    __slots__ = ("eng", "fn", "deps", "needs_sig", "sigval", "dma_slot", "dma_cnt", "dma_wait", "is_out")

    def __init__(self, eng, fn):
        self.eng = eng
        self.fn = fn
        self.deps = ()
        self.needs_sig = False
        self.sigval = 0
        self.dma_slot = None
        self.dma_cnt = 0
        self.is_out = False


class Prog:
    ENGS = ("pe", "act", "dve", "pool", "sp")

    def __init__(self):
        self.streams = {e: [] for e in self.ENGS}
        self.last_w = {}
        self.readers = {}
        self.slot_cnt = {}

    def op(self, eng, fn, reads=(), writes=(), dma_slot=None, is_out=False, dma_batch=1):
        o = Op(eng, fn)
        deps = {}
        for k in reads:
            w = self.last_w.get(k)
            if w is not None:
                deps[id(w)] = w
        for k in writes:
            w = self.last_w.get(k)
            if w is not None:
                deps[id(w)] = w
            for r in self.readers.get(k, ()):
                deps[id(r)] = r
        o.deps = tuple(deps.values())
        for k in reads:
            self.readers.setdefault(k, []).append(o)
        for k in writes:
            self.last_w[k] = o
            self.readers[k] = []
        if dma_slot is not None:
            o.dma_slot = dma_slot
            n = self.slot_cnt.get(dma_slot, 0) + 1
            self.slot_cnt[dma_slot] = n
            o.dma_cnt = 16 * n
            o.dma_wait = 16 * (-(-n // dma_batch)) * dma_batch
            o.is_out = is_out
        self.streams[eng].append(o)
        return o

    def finalize(self):
        for e in self.ENGS:
            for o in self.streams[e]:
                for d in o.deps:
                    if d.dma_slot is not None:
                        continue
                    if d.eng == "pe" and o.eng == "pe" and o.dma_slot is None:
                        continue
                    d.needs_sig = True
        for e in self.ENGS:
            c = 0
            for o in self.streams[e]:
                if o.needs_sig and o.dma_slot is None:
                    c += 1
                    o.sigval = c

    def emit(self, engine_handle, ename, esem, slotsem):
        waited = {}
        for o in self.streams[ename]:
            need = {}
            for d in o.deps:
                if d.dma_slot is not None:
                    key = ("d", d.dma_slot)
                    sem, val = slotsem[d.dma_slot], d.dma_wait
                else:
                    if d.eng == "pe" and ename == "pe" and o.dma_slot is None:
                        continue
                    key = ("e", d.eng)
                    sem, val = esem[d.eng], d.sigval
                if val > need.get(key, (None, 0))[1]:
                    need[key] = (sem, val)
            for key, (sem, val) in need.items():
                if waited.get(key, 0) >= val:
                    continue
                engine_handle.wait_ge(sem, val)
                waited[key] = val
            ins = o.fn(engine_handle)
            if o.dma_slot is not None:
                ins.then_inc(slotsem[o.dma_slot], 16)
            elif o.needs_sig:
                ins.then_inc(esem[ename], 1)
        if ename == "sp":
            for slot, n in self.slot_cnt.items():
                if slot[0] == "out":
                    engine_handle.wait_ge(slotsem[slot], 16 * n)


def rap(base, dims):
    p = base.ap[0]
    return bass.AP(base.tensor, base.offset, [[p[0], p[1]]] + [list(d) for d in dims])


def build_program():
    nc = bass.Bass("TRN2", target_bir_lowering=False)

    def din(name, shape, dt=F32):
        return nc.dram_tensor(name, shape, dt, kind="ExternalInput").ap()

    x_d = din("x", [NB, S, D])
    pos_d = din("pos", [NB, S], I32)
    cT_d = din("cT", [128, 32])
    wada_d = din("w_ada", [1024, 3072])
    bada_d = din("b_ada", [1, 3072])
    ngT_d = din("ngT", [128, 8])
    win_d = din("w_in", [1024, 2464])
    qgT_d = din("qgT", [128, 3])
    kvgT_d = din("kvgT", [128, 2])
    wuq_d = din("w_uq", [384, 768])
    wukv_d = din("w_ukv", [256, 1024])
    sinks_d = din("sinks", [1, 8])
    wout_d = din("w_out", [1024, 1024])
    fg_d = din("fgain", [1, 1024])
    cf_d = din("cf", [128, 40])
    cbf_d = din("cbf", [128, 896])
    out_d = nc.dram_tensor("out", [NB, S, D], F32, kind="ExternalOutput").ap()
    scr_d = nc.dram_tensor("scr", [4, 3072], F32, kind="Internal").ap()

    P = Prog()

    with ExitStack() as es:
        E = es.enter_context

        def T(name, shape, dt):
            return E(nc.sbuf_tensor("sb_" + name, shape, dt))

        win_b = T("win_b", [128, 8, NWIN], BF16)
        wuq_b = T("wuq_b", [128, 3, 1024], BF16)
        wukvK = T("wukvK", [128, 2, 512], BF16)
        wukvV = T("wukvV", [128, 2, 512], BF16)
        wout_bg = T("wout_bg", [128, 8, 1024], BF16)
        cbf = T("cbf", [128, 896], BF16)
        cf = T("cf", [128, 40], F32)
        fg_bc = T("fg_bc", [128, 1024], F32)
        small = T("small", [128, 160], F32)
        smalli = T("smalli", [128, 2], I32)
        arena = T("arena", [128, 14336], F32)
        arena_b = arena.bitcast(BF16)
        KsT = T("KsT", [128, 2, 640], BF16)
        VsS = T("VsS", [128, 5, 192], BF16)
        hT = T("hT", [128, 8, 512], BF16)
        QT = T("QT", [128, 8, 512], BF16)
        QsT = T("QsT", [128, 2 * 4 * 4 * 128], BF16)
        sg = T("sg", [128, 8, 512], BF16)
        sqzn = T("sqzn", [128, 5, 512], BF16)
        sqzn_f = sqzn.bitcast(F32)
        u_t = T("u_t", [128, 512], BF16)
        PTt = T("PTt", [128, 4, 512], BF16)
        PT = [PTt[:, i, :] for i in range(4)]
        PTf = PTt.bitcast(F32)
        rstd_q = rap(PTf[:, 0, :], [[1, 512]])
        rstd_kv = rap(PTf[:, 2, :], [[1, 512]])
        ogT = T("ogT", [128, 8, 512], BF16)
        rden = [T(f"rden{i}", [128, 512], F32) for i in range(2)]
        CS = T("CS", [128, 512], F32)
        posi = T("posi", [128, 512], I32)
        posi_f = posi.bitcast(F32)
        posf = T("posf", [128, 512], F32)
        ALB = T("ALB", [128, 512], BF16)
        xbuf = [T(f"xbuf{i}", [128, 1024], F32) for i in range(3)]
        xn = T("xn", [128, 1024], BF16)

        ps = [E(nc.psum_tensor(f"ps{i}", [128, 512], F32)) for i in range(8)]
        ps_bf = [p.bitcast(BF16) for p in ps]

        ident = cbf[:, 0:128]
        ones_b = cbf[:, 128:256]
        maskC = cbf[:, 256:384]
        maskP = cbf[:, 384:512]
        sel2 = cbf[:, 512:640]
        sinkL = [cbf[:, 640:768], cbf[:, 768:896]]

        def kT(h, a, b):
            return arena_b[:, h * 2048 + a:h * 2048 + b]

        VOFF = 16384

        def vst(blk, h):
            o = VOFF + blk * 768 + (h // 2) * 192 + (h % 2) * 64
            return arena_b[:, o:o + 128]

        stage = [arena[:, i * 3072:(i + 1) * 3072] for i in range(3)]
        modtok = arena[:, 9216:12288]

        A_all = small[:, 0:32]
        Sh_all = small[:, 32:64]
        ngT = small[:, 64:72]
        qgT = small[:, 72:75]
        nqgT = small[:, 75:78]
        kvgT = small[:, 78:80]
        esink = small[:, 80:88]
        cactT = small[:, 96:128]
        ssA = small[:, 128:132]
        ssF = small[:, 132:136]
        pos0f = small[:, 136:137]
        eskv = small[:, 140:148]
        eskf = small[:, 148:156]
        gate_bc = rap(sqzn_f[:, 0, :], [[1, 1024]])

        bank_ctr = [0]

        def nb(pool=8):
            i = bank_ctr[0] % pool
            bank_ctr[0] += 1
            return i

        def BK(i):
            return ("ps", i)

        ALLSTG = ("stgall",)

        P.op("pool", lambda e: e.dma_start(out=cf[:], in_=cf_d), writes=["cf"], dma_slot=("ld", "cf"))
        P.op("pool", lambda e: e.dma_start(out=rden[0][:, 0:448], in_=cbf_d[:, 0:448]),
             writes=[("rden", 0)], dma_slot=("ld", "cb0"))
        P.op("pool", lambda e: e.dma_start(out=rden[1][:, 0:448], in_=cbf_d[:, 448:896]),
             writes=[("rden", 1)], dma_slot=("ld", "cb1"))
        P.op("dve", lambda e: e.tensor_copy(out=cbf[:, 0:448], in_=rden[0][:, 0:448]),
             reads=[("rden", 0)], writes=["cbf"])
        P.op("dve", lambda e: e.tensor_copy(out=cbf[:, 448:896], in_=rden[1][:, 0:384 + 64]),
             reads=[("rden", 1)], writes=["cbf"])
        P.op("sp", lambda e: e.dma_start(out=cactT, in_=cT_d), writes=["cact"], dma_slot=("ld", "s0"))
        P.op("pool", lambda e: e.dma_start(out=ngT, in_=ngT_d), writes=["ngT"], dma_slot=("ld", "s1"))
        P.op("pool", lambda e: e.dma_start(out=qgT, in_=qgT_d), writes=["qgT"], dma_slot=("ld", "s2"))
        P.op("pool", lambda e: e.dma_start(out=kvgT, in_=kvgT_d), writes=["kvgT"], dma_slot=("ld", "s3"))
        P.op("pool", lambda e: e.dma_start(out=esink, in_=sinks_d.partition_broadcast(128)),
             writes=["esink"], dma_slot=("ld", "s4"))
        P.op("pool", lambda e: e.dma_start(out=fg_bc[:], in_=fg_d.partition_broadcast(128)),
             writes=["fg"], dma_slot=("ld", "s5"))
        P.op("pool", lambda e: e.dma_start(out=modtok[0:4, :], in_=bada_d.partition_broadcast(4)),
             writes=["modtok"], dma_slot=("ld", "s6"))
        P.op("act", lambda e: e.activation(out=cactT, in_=cactT, func=AF.Silu), reads=[], writes=["cact"])
        P.op("act", lambda e: e.activation(out=esink, in_=esink, func=AF.Exp), writes=["esink"])
        P.op("dve", lambda e: e.tensor_scalar(out=nqgT, in0=qgT, scalar1=-1.0, scalar2=None, op0=ALU.mult),
             reads=["qgT"], writes=["nqgT"])

        P.op("pool", lambda e: e.memset(u_t[0:64, :], 0.0), writes=["eskhl"])
        P.op("pool", lambda e: e.memset(KsT[64:128, :, :], 0.0), writes=[("KsTa", 0), ("KsTa", 1)])
        P.op("pool", lambda e: e.memset(QsT[64:128, :], 0.0), writes=[("QsTa", h) for h in range(8)])
        for kv in range(2):
            r = slice(kv * 32, kv * 32 + 2)
            P.op("dve", (lambda e, r=r: e.tensor_copy(out=u_t[r, 0:8], in_=esink[r, :])),
                 reads=["esink"], writes=["eskhl"])
            P.op("dve", (lambda e, r=r: e.tensor_copy(out=eskf[r, :], in_=u_t[r, 0:8])),
                 reads=["eskhl"], writes=["eskf"])
            P.op("dve", (lambda e, r=r: e.scalar_tensor_tensor(out=eskv[r, :], in0=eskf[r, :], scalar=cf[r, 33:34],
                                                              in1=esink[r, :], op0=ALU.mult, op1=ALU.add)),
                 reads=["eskf", "esink", "cf"], writes=["eskv"])
            P.op("dve", (lambda e, r=r, kv=kv: e.tensor_copy(out=rap(u_t[r, 0:512], [[128, 4], [1, 128]]),
                                                             in_=eskv[r, kv * 4:kv * 4 + 4].to_broadcast([2, 4, 128]))),
                 reads=["eskv"], writes=["eskhl"])

        for kc in range(8):
            st = stage[kc % 3]
            P.op("sp", (lambda e, st=st, kc=kc: e.dma_start(out=st, in_=wada_d[kc * 128:(kc + 1) * 128, :])),
                 writes=[("stg", kc % 3)], dma_slot=("ld", "wada", kc % 3))
            for j in range(6):
                P.op("pe", (lambda e, st=st, kc=kc, j=j: e.matmul(
                    ps[j][0:4, :], lhsT=cactT[:, kc * 4:(kc + 1) * 4], rhs=st[:, j * 512:(j + 1) * 512],
                    start=(kc == 0), stop=(kc == 7))),
                    reads=["cact", ("stg", kc % 3), ALLSTG], writes=[BK(j)])
        for j in range(6):
            P.op("dve", (lambda e, j=j: e.tensor_tensor(
                out=modtok[0:4, j * 512:(j + 1) * 512], in0=ps[j][0:4, :],
                in1=modtok[0:4, j * 512:(j + 1) * 512], op=ALU.add)),
                reads=[ALLSTG], writes=[BK(j), "modtok"])
        P.op("sp", lambda e: e.dma_start(out=scr_d, in_=modtok[0:4, :]), reads=["modtok", ALLSTG],
             writes=["scr"], dma_slot=("ld", "scr"))
        for i in range(16):
            P.op("pe", (lambda e, i=i: e.transpose(out=ps[6][:, i * 4:(i + 1) * 4],
                                                   in_=modtok[0:4, i * 128:(i + 1) * 128],
                                                   identity=cf[0:4, 29:33])),
                 reads=["modtok", "cf", ALLSTG], writes=[BK(6)])
        P.op("dve", lambda e: e.tensor_copy(out=rap(Sh_all, [[8, 4], [1, 8]]),
                                            in_=rap(ps[6][:, 0:32], [[1, 4], [4, 8]])),
             writes=[BK(6), "Sh"])
        P.op("dve", lambda e: e.tensor_scalar(out=rap(A_all, [[8, 4], [1, 8]]),
                                              in0=rap(ps[6][:, 32:64], [[1, 4], [4, 8]]),
                                              scalar1=1.0, scalar2=None, op0=ALU.add),
             writes=[BK(6), "A"])
        P.op("dve", lambda e: e.tensor_tensor(out=rap(A_all, [[8, 4], [1, 8]]), in0=rap(A_all, [[8, 4], [1, 8]]),
                                              in1=rap(ngT, [[0, 4], [1, 8]]), op=ALU.mult),
             reads=["ngT"], writes=["A"])

        cast_engs = ["dve", "act"]
        ce = [0]

        def cast(out_ap, in_ap, reads, writes, neg=False):
            eng = cast_engs[ce[0] % 2]
            ce[0] += 1
            if neg:
                eng = "dve"
                P.op(eng, lambda e: e.tensor_scalar(out=out_ap, in0=in_ap, scalar1=-1.0, scalar2=None, op0=ALU.mult),
                     reads=reads, writes=writes)
            elif eng == "act":
                P.op(eng, lambda e: e.activation(out=out_ap, in_=in_ap, func=AF.Copy), reads=reads, writes=writes)
            else:
                P.op(eng, lambda e: e.tensor_copy(out=out_ap, in_=in_ap), reads=reads, writes=writes)

        for kc in range(8):
            si = kc % 3
            st = stage[si]
            P.op("sp", (lambda e, st=st, kc=kc: e.dma_start(out=st[:, 0:2464], in_=win_d[kc * 128:(kc + 1) * 128, :])),
                 writes=[("stg", si)], dma_slot=("ld", "stg", si))
            rk = [("stg", si), ALLSTG]
            wk = [("win", kc)]
            cast(win_b[:, kc, 0:640], st[:, 0:640], rk, wk)
            cast(win_b[:, kc, 640:704], st[:, 1696:1760], rk, wk)
            cast(win_b[:, kc, 704:736], st[:, 640:672], rk, wk)
            cast(win_b[:, kc, 736:752], st[:, 656:672], rk, wk, neg=True)
            cast(win_b[:, kc, 752:768], st[:, 640:656], rk, wk)
            cast(win_b[:, kc, 768:832], st[:, 1760:1824], rk, wk)
            cast(win_b[:, kc, 832:1856], st[:, 672:1696], rk, wk)
            cast(win_b[:, kc, 1856:2496], st[:, 1824:2464], rk, wk)

        P.op("sp", lambda e: e.dma_start(out=rap(stage[2][:, 0:2304], [[768, 3], [1, 768]]),
                                         in_=wuq_d.rearrange("(k p) n -> p k n", p=128)),
             writes=[("stg", 2)], dma_slot=("ld", "stg", 2))
        for kc in range(3):
            src = stage[2][:, kc * 768:(kc + 1) * 768]
            dst = wuq_b[:, kc, :]
            rk = [("stg", 2), ALLSTG, "qgT", "nqgT"]
            P.op("dve", (lambda e, src=src, dst=dst, kc=kc: e.tensor_scalar(
                out=rap(dst[:, 0:96], [[128, 8], [1, 96]]), in0=rap(src[:, 0:96], [[96, 8], [1, 96]]),
                scalar1=qgT[:, kc:kc + 1], scalar2=None, op0=ALU.mult)), reads=rk, writes=["wuq"])
            P.op("dve", (lambda e, src=src, dst=dst, kc=kc: e.tensor_scalar(
                out=rap(dst[:, 96:112], [[128, 8], [1, 16]]), in0=rap(src[:, 80:96], [[96, 8], [1, 16]]),
                scalar1=nqgT[:, kc:kc + 1], scalar2=None, op0=ALU.mult)), reads=rk, writes=["wuq"])
            P.op("dve", (lambda e, src=src, dst=dst, kc=kc: e.tensor_scalar(
                out=rap(dst[:, 112:128], [[128, 8], [1, 16]]), in0=rap(src[:, 64:80], [[96, 8], [1, 16]]),
                scalar1=qgT[:, kc:kc + 1], scalar2=None, op0=ALU.mult)), reads=rk, writes=["wuq"])
        P.op("sp", lambda e: e.dma_start(out=rap(stage[0][:, 0:2048], [[1024, 2], [1, 1024]]),
                                         in_=wukv_d.rearrange("(k p) n -> p k n", p=128)),
             writes=[("stg", 0)], dma_slot=("ld", "stg", 0))
        for kc in range(2):
            src = stage[0][:, kc * 1024:(kc + 1) * 1024]
            rk = [("stg", 0), ALLSTG, "kvgT"]
            P.op("dve", (lambda e, src=src, kc=kc: e.tensor_scalar(
                out=rap(wukvK[:, kc, 0:64], [[64, 8], [1, 64]]), in0=rap(src[:, 0:64], [[128, 8], [1, 64]]),
                scalar1=kvgT[:, kc:kc + 1], scalar2=None, op0=ALU.mult)), reads=rk, writes=["wukv"])
            P.op("dve", (lambda e, src=src, kc=kc: e.tensor_scalar(
                out=rap(wukvV[:, kc, 0:64], [[64, 8], [1, 64]]), in0=rap(src[:, 64:128], [[128, 8], [1, 64]]),
                scalar1=kvgT[:, kc:kc + 1], scalar2=None, op0=ALU.mult)), reads=rk, writes=["wukv"])

        P.op("dve", lambda e: e.memset(ALB[:, 0:1], 0.0), writes=[ALLSTG, "ALB"])
        P.op("pool", lambda e: e.memset(rap(arena_b[:, VOFF + 64:VOFF + 128], [[192, 64], [1, 64]]), 1.0),
             reads=[ALLSTG], writes=["vones"])
        P.op("pool", lambda e: e.memset(rap(VsS[:, 0, 64:128], [[192, 5], [1, 64]]), 1.0), writes=["vsones"])

        pt_ctr = [0]
        rd_ctr = [0]
        abank_ctr = [0]

        def prep_pos_dma(b):
            P.op("pool", (lambda e: e.dma_start(out=smalli[:, 0:1], in_=pos_d[b:b + 1, 0:1].partition_broadcast(128))),
                 writes=["pos0i"], dma_slot=("ld", "pos0"))

        def prep_pos_cvt(b):
            P.op("dve", lambda e: e.tensor_copy(out=pos0f, in_=smalli[:, 0:1]), reads=["pos0i"], writes=["pos0f"])

        def prep_pos(b):
            prep_pos_dma(b)
            prep_pos_cvt(b)

        xctr = [0]

        def next_xbuf():
            i = xctr[0] % 3
            xctr[0] += 1
            return i

        gate_og = rap(ogT.bitcast(F32)[:, 0, :], [[1, 1024]])
        og_keys = [("og", i, q) for i in range(4) for q in range(4)]
        qstage = [rap(QT.bitcast(F32)[:, 0, :], [[1, 1024]]), rap(QT.bitcast(F32)[:, 4, :], [[1, 1024]])]
        qst_keys = [[("QT", i) for i in range(4)] + [("QTlo", i) for i in range(4)], [("QT", i) for i in range(4, 8)] + [("QTlo", i) for i in range(4, 8)]]

        def wout_gate_dma(b):
            P.op("sp", (lambda e: e.dma_start(out=gate_og, in_=scr_d[b:b + 1, 2048:3072].partition_broadcast(128))),
                 reads=["scr"], writes=og_keys, dma_slot=("ld", "gate"))

        def wout_load(kc):
            P.op("sp", (lambda e: e.dma_start(out=qstage[kc % 2], in_=wout_d[kc * 128:(kc + 1) * 128, :])),
                 writes=qst_keys[kc % 2], dma_slot=("ld", "qst", kc % 2))

        def wout_mul(kc):
            P.op("dve", (lambda e: e.tensor_tensor(out=wout_bg[:, kc, :], in0=qstage[kc % 2], in1=gate_og, op=ALU.mult)),
                 reads=qst_keys[kc % 2] + og_keys, writes=[("wout", kc)])

        def prep_wout(b):
            wout_gate_dma(b)
            wout_load(0)
            wout_load(1)
            for kc in range(8):
                wout_mul(kc)
                if kc + 2 < 8:
                    wout_load(kc + 2)

        def P1_dma(b, c):
            t0 = c * CH
            P.op("pool", (lambda e: e.dma_start(out=posi[:], in_=pos_d[b:b + 1, t0:t0 + CH].partition_broadcast(128))),
                 writes=["posi"], dma_slot=("ld", "posi"))

        def P1_cvt(b, c):
            P.op("dve", lambda e: e.tensor_copy(out=posf[:, :], in_=posi[:, :]), reads=["posi"], writes=["posf"])

        def P1_a(b, c):
            P1_dma(b, c)
            P1_cvt(b, c)

        def CSR(par):
            return slice(64, 128) if par == 0 else slice(0, 64)

        def P1_b(b, c, par=0):
            R = CSR(par)
            P.op("dve", lambda e: e.tensor_scalar(out=CS[R, :], in0=posf[R, :], scalar1=cf[R, 0:1],
                                                  scalar2=cf[R, 1:2], op0=ALU.mult, op1=ALU.add),
                 reads=["posf", "cf"], writes=[("CS", par)])
            P.op("dve", lambda e: e.tensor_copy(out=posi[R, :], in_=CS[R, :]), reads=[("CS", par)], writes=["posi"])
            P.op("dve", lambda e: e.tensor_copy(out=posi_f[R, :], in_=posi[R, :]), writes=["posi"])

        def P1_sub(b, c, par=0):
            R = CSR(par)
            P.op("dve", lambda e: e.tensor_tensor(out=CS[R, :], in0=CS[R, :], in1=posi_f[R, :], op=ALU.subtract),
                 reads=["posi"], writes=[("CS", par)])

        def P1_sin(b, c, par=0):
            R = CSR(par)
            P.op("act", lambda e: e.activation(out=CS[R, :], in_=CS[R, :], func=AF.Sin, scale=TWO_PI),
                 writes=[("CS", par)])

        def P1_c(b, c, par=0):
            P1_sub(b, c, par)
            P1_sin(b, c, par)

        def P1_d(b, c):
            R = slice(0, 36)
            P.op("dve", lambda e: e.tensor_scalar(out=posf[R, :], in0=posf[R, :], scalar1=pos0f[R, :],
                                                  scalar2=None, op0=ALU.subtract),
                 reads=["pos0f"], writes=["posf"])
            P.op("dve", lambda e: e.tensor_scalar(out=posi_f[R, :], in0=posf[R, :], scalar1=-63.5,
                                                  scalar2=1.0 / 128.0, op0=ALU.add, op1=ALU.mult),
                 reads=["posf"], writes=["posi"])
            P.op("dve", lambda e: e.tensor_copy(out=posi[R, :], in_=posi_f[R, :]), writes=["posi"])
            P.op("dve", lambda e: e.tensor_copy(out=posi_f[R, :], in_=posi[R, :]), writes=["posi"])
            P.op("dve", lambda e: e.scalar_tensor_tensor(out=posf[R, :], in0=posi_f[R, :], scalar=-128.0,
                                                         in1=posf[R, :], op0=ALU.mult, op1=ALU.add),
                 reads=["posi"], writes=["posf"])

        def P2_a(b, c):
            R = slice(0, 36)
            P.op("dve", lambda e: e.tensor_scalar(out=ALB[R, :], in0=posf[R, :], scalar1=cf[R, 35:36],
                                                  scalar2=cf[R, 37:38], op0=ALU.mult, op1=ALU.add),
                 reads=["posf", "cf"], writes=["ALB"])
            P.op("dve", lambda e: e.scalar_tensor_tensor(out=ALB[R, :], in0=posi_f[R, :], scalar=cf[R, 36:37],
                                                         in1=ALB[R, :], op0=ALU.mult, op1=ALU.add),
                 reads=["posi", "cf"], writes=["ALB"])

        def P2_b(b, c):
            for kv in range(2):
                P.op("pool", (lambda e, kv=kv: e.dma_start(out=KsT[64:68, kv, 128:640], in_=ALB[32:36, :])),
                     reads=["ALB"], writes=[("KsTa", kv)], dma_slot=("ld", "alb"), dma_batch=10)
            for h in range(8):
                kv, g = h // 4, h % 4
                slot = SLOT_OF_G[g]
                o = (kv * 16 + slot) * 128
                P.op("pool", (lambda e, h=h, o=o: e.dma_start(out=rap(QsT[64:68, o:o + 128], [[512, 4], [1, 128]]),
                                                            in_=rap(ALB[h * 4:h * 4 + 4, 0:128], [[128, 4], [1, 128]]))),
                     reads=["ALB"], writes=[("QsTa", h)] + (["albdone"] if h == 7 else []), dma_slot=("ld", "alb"), dma_batch=10)

        a_xb = {}

        def phase_A_load(b, c, tb):
            xi = next_xbuf()
            a_xb[(b, c, tb)] = xi
            xb = xbuf[xi]
            r0 = c * CH + tb * 128
            P.op("sp", (lambda e: e.dma_start(out=xb[:], in_=x_d[b, r0:r0 + 128, :])),
                 writes=[("xbuf", xi)], dma_slot=("ld", "x", xi))

        a_bk = {}

        def A_stt(b, c, tb):
            xi = a_xb[(b, c, tb)]
            xb = xbuf[xi]
            bk = 4 + (abank_ctr[0] % 2)
            abank_ctr[0] += 1
            a_bk[(b, c, tb)] = bk
            P.op("dve", (lambda e: e.scalar_tensor_tensor(out=rap(small[:, 157:158], [[0, 1024]]), in0=xb[:], scalar=1.0, in1=xb[:],
                                                          op0=ALU.mult, op1=ALU.mult, accum_out=ssA[:, tb:tb + 1])),
                 reads=[("xbuf", xi)], writes=["junk1", ("ssA", tb)])

        def A_lnexp(b, c, tb):
            P.op("act", (lambda e: e.activation(out=ssA[:, tb:tb + 1], in_=ssA[:, tb:tb + 1], func=AF.Ln,
                                                bias=cf[:, 34:35], scale=1.0 / D)),
                 reads=["cf"], writes=[("ssA", tb)])
            P.op("act", (lambda e: e.activation(out=ssA[:, tb:tb + 1], in_=ssA[:, tb:tb + 1], func=AF.Exp, scale=-0.5)),
                 writes=[("ssA", tb)])

        def A_scale(b, c, tb):
            xi = a_xb[(b, c, tb)]
            xb = xbuf[xi]
            P.op("dve", (lambda e: e.tensor_scalar(out=xn[:], in0=xb[:], scalar1=ssA[:, tb:tb + 1], scalar2=None, op0=ALU.mult)),
                 reads=[("xbuf", xi), ("ssA", tb)], writes=["xn"])

        def A_tr(b, c, tb):
            bk = a_bk[(b, c, tb)]
            for kc in range(8):
                P.op("pe", (lambda e, kc=kc: e.transpose(out=ps_bf[bk][:, kc * 128:(kc + 1) * 128],
                                                         in_=xn[:, kc * 128:(kc + 1) * 128], identity=ident)),
                     reads=["xn", "cbf"], writes=[BK(bk)])

        def A_evac(b, c, tb):
            bk = a_bk[(b, c, tb)]
            for kc in range(8):
                P.op("dve", (lambda e, kc=kc: e.tensor_scalar(
                    out=hT[:, kc, tb * 128:(tb + 1) * 128], in0=ps_bf[bk][:, kc * 128:(kc + 1) * 128],
                    scalar1=A_all[:, b * 8 + kc:b * 8 + kc + 1], scalar2=Sh_all[:, b * 8 + kc:b * 8 + kc + 1],
                    op0=ALU.mult, op1=ALU.add)),
                    reads=["A", "Sh"], writes=[BK(bk), ("hT", kc)])

        BBANKS = [0, 1, 2, 3, 6, 7]
        bb_ctr = [0]

        def nbB():
            i = BBANKS[bb_ctr[0] % 6]
            bb_ctr[0] += 1
            return i

        def phase_B(b, c, hooks, par=0):
            RC = CSR(par)
            t0 = c * CH
            tick_i = [0]
            bb_ctr[0] = 0

            def tick():
                for f in hooks.pop(tick_i[0], ()):
                    f()
                tick_i[0] += 1

            for f in hooks.pop(-1, ()):
                f()
            if c > 0:
                P.op("pool", lambda e: e.tensor_copy(out=KsT[0:68, :, 0:128], in_=KsT[0:68, :, 512:640]),
                     reads=["albdone", ("KsT", 4), ("KsTb", 1), ("KsTa", 0), ("KsTa", 1)], writes=[("KsT", 0)])
                P.op("pool", lambda e: e.tensor_copy(out=VsS[:, 0, :], in_=VsS[:, 4, :]),
                     reads=[("VsS", 4)], writes=[("VsS", 0)])

            def proj(c0, m, bank):
                for kc in range(8):
                    P.op("pe", (lambda e, kc=kc: e.matmul(ps[bank][0:m, :], lhsT=win_b[:, kc, c0:c0 + m], rhs=hT[:, kc, :],
                                                          start=(kc == 0), stop=(kc == 7))),
                         reads=[("win", kc), ("hT", kc)], writes=[BK(bank)])

            zb = []
            for zi in range(5):
                bk = nbB()
                zb.append(bk)
                proj(C_ZQ + 128 * zi, 128, bk)
                P.op("act", (lambda e, zi=zi, bk=bk: e.activation(out=sqzn[:, zi, :], in_=ps[bk][:, :], func=AF.Square)),
                     writes=[BK(bk), ("sqzn", zi)])
                tick()
            bq, bkv = 4, 5
            for i in range(3):
                P.op("pe", (lambda e, i=i: e.matmul(ps[bq][:, :], lhsT=ones_b, rhs=sqzn[:, i, :], start=(i == 0), stop=(i == 2))),
                     reads=[("sqzn", i), "cbf"], writes=[BK(bq)])
            for i in range(2):
                P.op("pe", (lambda e, i=i: e.matmul(ps[bkv][:, :], lhsT=ones_b, rhs=sqzn[:, 3 + i, :], start=(i == 0), stop=(i == 1))),
                     reads=[("sqzn", 3 + i), "cbf"], writes=[BK(bkv)])
            kq = [("PT", 0), ("PT", 1)]
            kkv = [("PT", 2), ("PT", 3)]
            P.op("act", lambda e: e.activation(out=rstd_q, in_=ps[bq][:, :], func=AF.Ln, bias=cf[:, 34:35], scale=1.0 / 384.0),
                 reads=["cf"], writes=[BK(bq)] + kq)
            P.op("act", lambda e: e.activation(out=rstd_q, in_=rstd_q, func=AF.Exp, scale=-0.5), writes=kq)
            P.op("act", lambda e: e.activation(out=rstd_kv, in_=ps[bkv][:, :], func=AF.Ln, bias=cf[:, 34:35], scale=1.0 / 256.0),
                 reads=["cf"], writes=[BK(bkv)] + kkv)
            P.op("act", lambda e: e.activation(out=rstd_kv, in_=rstd_kv, func=AF.Exp, scale=-0.5), writes=kkv)
            for zi in range(5):
                rs_t, rk = (rstd_q, kq) if zi < 3 else (rstd_kv, kkv)
                P.op("dve", (lambda e, zi=zi, rs_t=rs_t, zbk=zb[zi]: e.tensor_tensor(out=sqzn[:, zi, :], in0=ps[zbk][:, :], in1=rs_t,
                                                                                     op=ALU.mult)),
                     reads=rk, writes=[BK(zb[zi]), ("sqzn", zi)])
            tick()

            bk = nbB()
            proj(C_K1, 128, bk)
            P.op("dve", (lambda e, bk=bk: e.tensor_copy(out=KsT[0:64, 0, 128:640], in_=ps[bk][0:64, :])),
                 writes=[BK(bk), ("KsT", 1), ("KsT", 2), ("KsT", 3), ("KsT", 4)])
            P.op("dve", (lambda e, bk=bk: e.tensor_tensor(out=u_t[64:128, :], in0=ps[bk][64:128, :], in1=CS[RC, :], op=ALU.mult)),
                 reads=[("CS", par)], writes=[BK(bk), "u"])
            tick()
            bk = nbB()
            proj(C_K2, 64, bk)
            P.op("dve", (lambda e, bk=bk: e.tensor_copy(out=KsT[0:64, 1, 128:640], in_=ps[bk][0:64, :])),
                 writes=[BK(bk), ("KsTb", 1)])
            tick()
            for i in range(8):
                bk = nbB()
                proj((C_GM if i < 4 else C_GS) + 128 * (i % 4), 128, bk)
                P.op("act", (lambda e, i=i, bk=bk: e.activation(out=sg[:, i, :], in_=ps[bk][:, :], func=AF.Silu)),
                     writes=[BK(bk), ("sg", i)])
                tick()
            for j in range(4):
                bk = nbB()
                proj(C_QS + 128 * j, 128, bk)
                for half in range(2):
                    h = 2 * j + half
                    kv, g = h // 4, h % 4
                    slot = SLOT_OF_G[g]
                    o = (kv * 16 + slot) * 128
                    P.op("dve", (lambda e, bk=bk, half=half, o=o: e.tensor_copy(
                        out=rap(QsT[0:64, o:o + 128], [[512, 4], [1, 128]]),
                        in_=rap(ps[bk][half * 64:half * 64 + 64, 0:128], [[128, 4], [1, 128]]))),
                        writes=[BK(bk), ("QsT", h)])
                tick()
            bk = nbB()
            for tb in range(4):
                for kc in range(8):
                    P.op("pe", (lambda e, kc=kc, tb=tb, bk=bk: e.matmul(
                        ps[bk][:, tb * 128:(tb + 1) * 128], lhsT=hT[:, kc, tb * 128:(tb + 1) * 128],
                        rhs=win_b[:, kc, C_VS:C_VS + 128], start=(kc == 0), stop=(kc == 7))),
                        reads=[("win", kc), ("hT", kc)], writes=[BK(bk)])
            P.op("dve", (lambda e, bk=bk: e.tensor_copy(out=rap(VsS[:, 1, 0:64], [[192, 4], [1, 64]]),
                                                        in_=rap(ps[bk][:, 0:64], [[128, 4], [1, 64]]))),
                 reads=["vsones"], writes=[BK(bk)] + [("VsS", s) for s in range(1, 5)])
            P.op("dve", (lambda e, bk=bk: e.tensor_copy(out=rap(VsS[:, 1, 128:192], [[192, 4], [1, 64]]),
                                                        in_=rap(ps[bk][:, 64:128], [[128, 4], [1, 64]]))),
                 writes=[BK(bk)] + [("VsS", s) for s in range(1, 5)])
            tick()

            for h in range(8):
                bk = nbB()
                for kc in range(3):
                    P.op("pe", (lambda e, h=h, kc=kc, bk=bk: e.matmul(
                        ps[bk][:, :], lhsT=wuq_b[:, kc, h * 128:(h + 1) * 128], rhs=sqzn[:, kc, :],
                        start=(kc == 0), stop=(kc == 2))),
                        reads=["wuq", ("sqzn", kc)], writes=[BK(bk)])
                P.op("act", (lambda e, h=h, bk=bk: e.activation(out=QT[0:64, h, :], in_=ps[bk][0:64, :], func=AF.Copy)),
                     reads=[BK(bk)], writes=[("QTlo", h)])
                P.op("dve", (lambda e, h=h, bk=bk: e.tensor_tensor(out=QT[64:128, h, :], in0=ps[bk][64:128, :], in1=CS[RC, :], op=ALU.mult)),
                     reads=[BK(bk), ("CS", par)], writes=[("QT", h)])
                tick()
            for h in range(8):
                bk = nbB()
                P.op("pe", (lambda e, bk=bk: e.matmul(ps[bk][:, :], lhsT=sel2, rhs=u_t[:, :], start=True, stop=True)),
                     reads=["u", "cbf", "eskhl"], writes=[BK(bk)])
                for kc in range(2):
                    P.op("pe", (lambda e, h=h, kc=kc, bk=bk: e.matmul(
                        ps[bk][0:64, :], lhsT=wukvK[:, kc, h * 64:(h + 1) * 64], rhs=sqzn[:, 3 + kc, :],
                        start=(kc == 0), stop=(kc == 1))),
                        reads=["wukv", ("sqzn", 3 + kc)], writes=[BK(bk)])
                P.op("act", (lambda e, h=h, bk=bk: e.activation(out=kT(h, t0, t0 + CH), in_=ps[bk][:, :], func=AF.Copy)),
                     reads=[ALLSTG], writes=[BK(bk), ("kT", h, c)])
                tick()
            for tb in range(4):
                bk = nbB()
                blk = c * 4 + tb
                for kc in range(2):
                    P.op("pe", (lambda e, kc=kc, tb=tb, bk=bk: e.matmul(
                        ps[bk][:, :], lhsT=sqzn[:, 3 + kc, tb * 128:(tb + 1) * 128], rhs=wukvV[:, kc, :],
                        start=(kc == 0), stop=(kc == 1))),
                        reads=["wukv", ("sqzn", 3 + kc)], writes=[BK(bk)])
                ob = VOFF + blk * 768
                P.op("dve", (lambda e, bk=bk, ob=ob: e.tensor_copy(
                    out=rap(arena_b[:, ob:ob + 64], [[192, 4], [1, 64]]), in_=rap(ps[bk][:, 0:64], [[128, 4], [1, 64]]))),
                    reads=[ALLSTG, "vones"], writes=[BK(bk), ("vst", blk)])
                P.op("dve", (lambda e, bk=bk, ob=ob: e.tensor_copy(
                    out=rap(arena_b[:, ob + 128:ob + 192], [[192, 4], [1, 64]]), in_=rap(ps[bk][:, 64:128], [[128, 4], [1, 64]]))),
                    reads=[ALLSTG], writes=[BK(bk), ("vst", blk)])
                tick()
            for k in sorted(hooks):
                for f in hooks[k]:
                    f()

        LOOK = 3
        OBANKS = [3, 6, 7]

        def attention(b, c, side):
            inject = {}
            nsteps = 8 * (4 * c + 4) + LOOK
            for k, f in enumerate(side):
                if f is not None:
                    inject.setdefault((k * nsteps) // max(len(side), 1), []).append(f)
            steps = []
            for h in range(8):
                nj = 4 * c + 4
                for j in range(nj):
                    steps.append((h, j, nj))
            info = {}

            def emit_qk(si):
                h, j, nj = steps[si]
                bk = nb(3)
                pt = pt_ctr[0] % 4
                pt_ctr[0] += 1
                i = j - 4 * c
                c0 = 128 * i if i >= 0 else 0
                diag = i >= 0
                P.op("pe", (lambda e: e.matmul(ps[bk][:, c0:512], lhsT=kT(h, j * 128, (j + 1) * 128), rhs=QT[:, h, c0:512],
                                               start=True, stop=(not diag))),
                     reads=[("kT", h, j // 4), ("QT", h), ("QTlo", h)], writes=[BK(bk)])
                if diag:
                    P.op("pe", (lambda e: e.matmul(ps[bk][:, c0:c0 + 128], lhsT=ident, rhs=maskC, start=False, stop=True)),
                         reads=["cbf"], writes=[BK(bk)])
                P.op("act", (lambda e: e.activation(out=PT[pt][:, c0:512], in_=ps[bk][:, c0:512], func=AF.Exp, scale=SC_MLA)),
                     writes=[BK(bk), ("PT", pt)])
                info[si] = (pt, c0)

            def emit_pv(si):
                h, j, nj = steps[si]
                pt, c0 = info[si]
                ob = OBANKS[h % 3]
                P.op("pe", (lambda e: e.matmul(ps[ob][:, c0:512], lhsT=vst(j, h), rhs=PT[pt][:, c0:512],
                                               start=(j == 0), stop=(j == nj - 1))),
                     reads=[("PT", pt), ("vst", j), "vones"], writes=[BK(ob)])
                if j == nj - 1:
                    rd = rd_ctr[0] % 2
                    rd_ctr[0] += 1
                    m = h // 2
                    if h % 2 == 0:
                        o_, den_, dst = slice(0, 64), slice(64, 128), slice(0, 64)
                    else:
                        o_, den_, dst = slice(64, 128), slice(0, 64), slice(64, 128)
                    if c == 3 or (c == 2 and h % 4 != 0) or (c == 1 and h % 4 == 3):
                        P.op("dve", (lambda e: e.reciprocal(out=ps[ob][den_, :], in_=ps[ob][den_, :])),
                             reads=[BK(ob)], writes=[("pden", ob)])
                    else:
                        P.op("act", (lambda e: e.activation(out=ps[ob][den_, :], in_=ps[ob][den_, :], func=AF.Ln)),
                             reads=[BK(ob)], writes=[("pden", ob)])
                        P.op("act", (lambda e: e.activation(out=ps[ob][den_, :], in_=ps[ob][den_, :], func=AF.Exp, scale=-1.0)),
                             reads=[BK(ob)], writes=[("pden", ob)])
                    P.op("dve", (lambda e: e.tensor_tensor(out=rden[rd][dst, :], in0=ps[ob][o_, :], in1=sg[dst, m, :], op=ALU.mult)),
                         reads=[BK(ob), ("sg", m)], writes=[("rden", rd)])
                    P.op("dve", (lambda e: e.tensor_tensor(out=ogT[dst, m, :], in0=ps[ob][den_, :], in1=rden[rd][dst, :], op=ALU.mult)),
                         reads=[BK(ob), ("pden", ob), ("rden", rd)], writes=[("og", m, q) for q in range(4)])

            sw = []
            for qb in range(4):
                for kv in range(2):
                    blk = 4 * c + qb
                    parts = ([] if blk == 0 else [(qb, True)]) + [(qb + 1, False)]
                    for pi, (slot, isprev) in enumerate(parts):
                        sw.append((kv, qb, slot, isprev, pi == 0, pi == len(parts) - 1))
            info2 = {}

            def sw_qk(si):
                kv, qb, slot, isprev, first, last = sw[si]
                bk = nb(3)
                pt = pt_ctr[0] % 4
                pt_ctr[0] += 1
                qo = (kv * 4 + qb) * 512
                P.op("pe", (lambda e: e.matmul(ps[bk][:, :], lhsT=KsT[:, kv, slot * 128:(slot + 1) * 128],
                                               rhs=QsT[:, qo:qo + 512], start=True, stop=False)),
                     reads=["albdone", ("KsT", slot), ("KsTa", kv), ("KsTb", 1)] + [("QsT", kv * 4 + g) for g in range(4)]
                     + [("QsTa", kv * 4 + g) for g in range(4)], writes=[BK(bk)])
                mk = maskP if isprev else maskC
                for g in range(4):
                    P.op("pe", (lambda e, g=g: e.matmul(ps[bk][:, g * 128:(g + 1) * 128], lhsT=ident, rhs=mk, start=False, stop=(g == 3))),
                         reads=["cbf"], writes=[BK(bk)])
                P.op("act", (lambda e: e.activation(out=PT[pt][:, :], in_=ps[bk][:, :], func=AF.Exp, scale=SC_SWA)),
                     writes=[BK(bk), ("PT", pt)])
                info2[si] = pt

            def sw_pv(si):
                kv, qb, slot, isprev, first, last = sw[si]
                pt = info2[si]
                ob = OBANKS[(qb * 2 + kv + 2) % 3]
                P.op("pe", (lambda e: e.matmul(ps[ob][:, :], lhsT=VsS[:, slot, kv * 64:kv * 64 + 128], rhs=PT[pt][:, :],
                                               start=first, stop=False)),
                     reads=[("PT", pt), ("VsS", slot), "vsones"], writes=[BK(ob)])
                if last:
                    P.op("pe", (lambda e: e.matmul(ps[ob][:, :], lhsT=sinkL[kv], rhs=u_t[:, :], start=False, stop=True)),
                         reads=["eskhl", "cbf", "u"], writes=[BK(ob)])
                    for par in range(2):
                        rd = rd_ctr[0] % 2
                        rd_ctr[0] += 1
                        cs_ = slice(par * 256, par * 256 + 256)
                        dst = slice(0, 64) if par == 0 else slice(64, 128)
                        if kv == 0:
                            o_, den_ = slice(0, 64), slice(64, 128)
                        else:
                            o_, den_ = slice(64, 128), slice(0, 64)
                        m0 = 4 + 2 * kv
                        if par == 0:
                            P.op("act", (lambda e, den_=den_: e.activation(out=ps[ob][den_, :], in_=ps[ob][den_, :], func=AF.Ln)),
                                 reads=[BK(ob)], writes=[("pden", ob)])
                            P.op("act", (lambda e, den_=den_: e.activation(out=ps[ob][den_, :], in_=ps[ob][den_, :], func=AF.Exp, scale=-1.0)),
                                 reads=[BK(ob)], writes=[("pden", ob)])
                        P.op("dve", (lambda e, rd=rd, cs_=cs_, dst=dst, o_=o_: e.tensor_tensor(
                            out=rap(rden[rd][dst, cs_], [[128, 2], [1, 128]]), in0=rap(ps[ob][o_, cs_], [[128, 2], [1, 128]]),
                            in1=rap(sg[dst, m0, qb * 128:(qb + 1) * 128], [[512, 2], [1, 128]]), op=ALU.mult)),
                            reads=[BK(ob), ("sg", m0), ("sg", m0 + 1)], writes=[("rden", rd)])
                        P.op("dve", (lambda e, rd=rd, cs_=cs_, dst=dst, den_=den_: e.tensor_tensor(
                            out=rap(ogT[dst, m0, qb * 128:(qb + 1) * 128], [[512, 2], [1, 128]]),
                            in0=rap(ps[ob][den_, cs_], [[128, 2], [1, 128]]), in1=rap(rden[rd][dst, cs_], [[128, 2], [1, 128]]),
                            op=ALU.mult)),
                            reads=[BK(ob), ("pden", ob), ("rden", rd)], writes=[("og", m0, qb), ("og", m0 + 1, qb)])

            items = [("m", i) for i in range(len(steps))] + [("s", i) for i in range(len(sw))]
            for k in range(len(items) + LOOK):
                if k < len(items):
                    kind, i = items[k]
                    (emit_qk if kind == "m" else sw_qk)(i)
                if k - LOOK >= 0:
                    kind, i = items[k - LOOK]
                    (emit_pv if kind == "m" else sw_pv)(i)
                for f in inject.pop(k, ()):
                    f()
            for k in sorted(inject):
                for f in inject[k]:
                    f()

        f_xb = {}

        def phase_F_load(b, c, tb):
            xi = next_xbuf()
            f_xb[(b, c, tb)] = xi
            xb = xbuf[xi]
            r0 = c * CH + tb * 128
            P.op("sp", (lambda e: e.dma_start(out=xb[:], in_=x_d[b, r0:r0 + 128, :])),
                 writes=[("xbuf", xi)], dma_slot=("ld", "x", xi))

        sgx = rap(sg.bitcast(F32)[:, 0, :], [[1, 1024]])
        sgx_keys = [("sg", i) for i in range(4)]

        def phase_F(b, c):
            t0 = c * CH
            for tb in range(3):
                if (b, c, tb) not in f_xb:
                    phase_F_load(b, c, tb)
            r3 = t0 + 3 * 128
            P.op("sp", (lambda e: e.dma_start(out=sgx, in_=x_d[b, r3:r3 + 128, :])),
                 writes=sgx_keys, dma_slot=("ld", "sgx"))
            for tb in range(4):
                if tb < 3:
                    xi = f_xb[(b, c, tb)]
                    xb = xbuf[xi][:, :]
                    xk = [("xbuf", xi)]
                    oslot = ("out", xi)
                else:
                    xb = sgx
                    xk = sgx_keys
                    oslot = ("out", 3)
                r0 = t0 + tb * 128
                fbk = [(0, 1), (2, 3), (6, 7), (4, 5)][tb]
                for mlo in (0, 4):
                    for nh in range(2):
                        bk = fbk[nh]
                        for m in range(mlo, mlo + 4):
                            P.op("pe", (lambda e, m=m, nh=nh, tb=tb, bk=bk: e.matmul(
                                ps[bk][:, :], lhsT=ogT[:, m, tb * 128:(tb + 1) * 128], rhs=wout_bg[:, m, nh * 512:(nh + 1) * 512],
                                start=(m == 0), stop=(m == 7))),
                                reads=[("og", m, tb), ("wout", m)], writes=[BK(bk)])
                for nh in range(2):
                    bk = fbk[nh]
                    P.op("dve", (lambda e, xb=xb, nh=nh, bk=bk: e.tensor_tensor(
                        out=xb[:, nh * 512:(nh + 1) * 512], in0=ps[bk][:, :], in1=xb[:, nh * 512:(nh + 1) * 512], op=ALU.add)),
                        writes=[BK(bk)] + xk)
                P.op("act", (lambda e, xb=xb, tb=tb: e.activation(out=xn[:], in_=xb, func=AF.Square, accum_out=ssF[:, tb:tb + 1])),
                     reads=xk, writes=["xn", ("ssF", tb)])
                P.op("act", (lambda e, tb=tb: e.activation(out=ssF[:, tb:tb + 1], in_=ssF[:, tb:tb + 1], func=AF.Ln,
                                                           bias=cf[:, 34:35], scale=1.0 / D)),
                     reads=["cf"], writes=[("ssF", tb)])
                P.op("act", (lambda e, tb=tb: e.activation(out=ssF[:, tb:tb + 1], in_=ssF[:, tb:tb + 1], func=AF.Exp, scale=-0.5)),
                     writes=[("ssF", tb)])
                P.op("dve", (lambda e, xb=xb, tb=tb: e.scalar_tensor_tensor(out=xb, in0=xb, scalar=ssF[:, tb:tb + 1], in1=fg_bc[:],
                                                                            op0=ALU.mult, op1=ALU.mult)),
                     reads=[("ssF", tb), "fg"], writes=xk)
                P.op("sp", (lambda e, xb=xb, r0=r0: e.dma_start(out=out_d[b, r0:r0 + 128, :], in_=xb)),
                     reads=xk, dma_slot=oslot, is_out=True)

        chunks = [(b, c) for b in range(NB) for c in range(NCH)]
        prep_pos(0)
        P1_a(0, 0); P1_b(0, 0, 0); P1_c(0, 0, 0); P1_d(0, 0)
        for tb in range(3):
            phase_A_load(0, 0, tb)
        for tb in range(3):
            A_stt(0, 0, tb)
        for tb in range(3):
            A_lnexp(0, 0, tb)
        A_scale(0, 0, 0); A_tr(0, 0, 0); phase_A_load(0, 0, 3); A_evac(0, 0, 0)
        A_scale(0, 0, 1); A_tr(0, 0, 1); A_stt(0, 0, 3); A_lnexp(0, 0, 3); A_evac(0, 0, 1)
        A_scale(0, 0, 2); A_tr(0, 0, 2); A_evac(0, 0, 2)
        A_scale(0, 0, 3); A_tr(0, 0, 3); A_evac(0, 0, 3)
        for idx, (b, c) in enumerate(chunks):
            hooks = {}

            def H(k, f):
                hooks.setdefault(k, []).append(f)

            par = idx % 2
            H(0, lambda b=b, c=c: P2_a(b, c))
            H(1, lambda b=b, c=c: P2_b(b, c))
            if idx + 1 < len(chunks):
                n1 = chunks[idx + 1]
                if n1[1] == 0:
                    H(-1, lambda n1=n1: prep_pos_dma(n1[0]))
                    H(6, lambda n1=n1: prep_pos_cvt(n1[0]))
                H(2, lambda n1=n1: P1_dma(*n1))
                H(6, lambda n1=n1: P1_cvt(*n1))
                H(8, lambda n1=n1, par=par: P1_b(n1[0], n1[1], 1 - par))
                H(9, lambda n1=n1, par=par: P1_sub(n1[0], n1[1], 1 - par))
                H(11, lambda n1=n1: P1_d(*n1))
                H(23, lambda n1=n1, par=par: P1_sin(n1[0], n1[1], 1 - par))
            if c == 0:
                H(-1, lambda b=b: (wout_gate_dma(b), wout_load(0), wout_load(1)))
                for kc in range(8):
                    H(5 + 2 * kc, lambda kc=kc: (wout_mul(kc), wout_load(kc + 2) if kc + 2 < 8 else None))
            if idx + 1 < len(chunks):
                n_ = chunks[idx + 1]
                H(-1, lambda n_=n_: [phase_A_load(n_[0], n_[1], t) for t in range(3)])
                sched = {10: [("stt", 0)], 12: [("stt", 1)], 16: [("ln", 0), ("ln", 1)], 17: [("sc", 0)],
                         19: [("tr", 0), ("ld", 3)], 20: [("sc", 1)], 22: [("tr", 1), ("fld", 0)], 25: [("ev", 0)], 30: [("ev", 1)]}
                fmap = {"stt": A_stt, "ln": A_lnexp, "sc": A_scale, "tr": A_tr, "ev": A_evac}
                for k, items in sched.items():
                    for kind, t in items:
                        if kind == "ld":
                            H(k, lambda n_=n_, t=t: phase_A_load(n_[0], n_[1], t))
                        elif kind == "fld":
                            H(k, lambda b=b, c=c, t=t: phase_F_load(b, c, t))
                        else:
                            H(k, lambda n_=n_, t=t, fn=fmap[kind]: fn(n_[0], n_[1], t))
            else:
                H(-1, lambda b=b, c=c: [phase_F_load(b, c, t) for t in range(3)])
            phase_B(b, c, hooks, par)
            NSLOT = 9
            slots = [[] for _ in range(NSLOT)]
            if idx + 1 < len(chunks):
                n_ = chunks[idx + 1]
                A2 = [(0, [("stt", 2)]), (1, [("ln", 2)]), (2, [("sc", 2)]), (3, [("tr", 2), ("stt", 3), ("fld", 1)]),
                      (4, [("ev", 2), ("ln", 3)]), (5, [("sc", 3)]), (6, [("tr", 3), ("fld", 2)]), (7, [("ev", 3)])]
                for k, items in A2:
                    for kind, t in items:
                        if kind == "fld":
                            slots[k].append(lambda b=b, c=c, t=t: phase_F_load(b, c, t))
                        else:
                            slots[k].append(lambda n_=n_, t=t, fn=fmap[kind]: fn(n_[0], n_[1], t))
            side = [(lambda fs=fs: [f() for f in fs]) if fs else None for fs in slots]
            attention(b, c, side)
            phase_F(b, c)

        P.finalize()
        esem = {e: E(nc.semaphore(f"s_{e}")) for e in Prog.ENGS}
        slotsem = {}
        for i, slot in enumerate(P.slot_cnt.keys()):
            slotsem[slot] = E(nc.semaphore(f"d_{i}"))
        block = E(nc.Block())

        @block.tensor
        def _(e):
            P.emit(e, "pe", esem, slotsem)

        @block.scalar
        def _(e):
            P.emit(e, "act", esem, slotsem)

        @block.vector
        def _(e):
            P.emit(e, "dve", esem, slotsem)

        @block.gpsimd
        def _(e):
            P.emit(e, "pool", esem, slotsem)

        @block.sync
        def _(e):
            P.emit(e, "sp", esem, slotsem)

    return nc


def _consts():
    cf = np.zeros((128, 40), np.float32)
    p = np.arange(128)
    inv = (10000.0 ** (-np.arange(0, 32, 2, dtype=np.float32) / 32.0)).astype(np.float32)
    cf[:, 0] = (inv[p % 16].astype(np.float64) / (2.0 * np.pi)).astype(np.float32)
    cf[:, 1] = np.where((p % 64) < 32, 0.25, 0.0)
    cf[64, 2] = 1.0
    cf[66, 3] = 1.0
    cf[65, 4] = 1.0
    cf[67, 4] = 1.0
    for h in range(8):
        sl = 2.0 ** (-(h + 1))
        cf[65, 5 + h] = -8.0 * sl
        cf[67, 13 + h] = -1024.0 * sl
        cf[64, 21 + h] = 8.0 * sl
        cf[66, 21 + h] = 1024.0 * sl
    cf[0:4, 29:33] = np.eye(4, dtype=np.float32)
    for h in range(8):
        sl = 2.0 ** (-(h + 1))
        cf[h * 4 + 0, 37] = 8.0 * sl
        cf[h * 4 + 1, 35] = -8.0 * sl
        cf[h * 4 + 2, 37] = 1024.0 * sl
        cf[h * 4 + 3, 36] = -1024.0 * sl
    cf[32, 35] = 1.0
    cf[33, 37] = 1.0
    cf[34, 36] = 1.0
    cf[35, 37] = 1.0
    cf[0, 33] = 0.0
    cf[1, 33] = -1.0
    cf[32, 33] = 0.0
    cf[33, 33] = -1.0
    cf[:, 34] = EPS
    cb = np.zeros((128, 896), np.float32)
    cb[:, 0:128] = np.eye(128)
    cb[:, 128:256] = 1.0
    s_ = np.arange(128)[:, None]
    t_ = np.arange(128)[None, :]
    cb[:, 256:384] = np.where(s_ <= t_, 0.0, NEG)
    cb[:, 384:512] = np.where(s_ > t_, 0.0, NEG)
    for k in range(64):
        cb[64 + k, 512 + 64 + (k % 32)] = 1.0
        cb[64 + k, 512 + 96 + (k % 32)] = 1.0
    cb[0:2, 640 + 64:640 + 128] = 1.0
    cb[32:34, 768:768 + 64] = 1.0
    return cf, cb


_NC_CACHE = {}


def kernel(x, c, positions, w_ada, b_ada, norm_gain, w_in, q_norm_gain, kv_norm_gain,
           w_uq, w_ukv, swa_sinks, w_out, final_gain):
    f32 = np.float32
    x = np.ascontiguousarray(np.asarray(x, f32))
    c = np.asarray(c, f32)
    positions = np.ascontiguousarray(np.asarray(positions, np.int32))
    if "nc" not in _NC_CACHE:
        _NC_CACHE["nc"] = build_program()
    nc = _NC_CACHE["nc"]
    cf, cb = _consts()
    perm = [0, 2, 1, 3, 4, 6, 5, 7]
    shared = {
        "w_ada": np.ascontiguousarray(np.asarray(w_ada, f32)[0]),
        "b_ada": np.ascontiguousarray(np.asarray(b_ada, f32)[0].reshape(1, 3072)),
        "ngT": np.ascontiguousarray(np.asarray(norm_gain, f32)[0].reshape(8, 128).T),
        "w_in": np.ascontiguousarray(np.asarray(w_in, f32)[0]),
        "qgT": np.ascontiguousarray(np.asarray(q_norm_gain, f32)[0].reshape(3, 128).T),
        "kvgT": np.ascontiguousarray(np.asarray(kv_norm_gain, f32)[0].reshape(2, 128).T),
        "w_uq": np.ascontiguousarray(np.asarray(w_uq, f32)[0]),
        "w_ukv": np.ascontiguousarray(np.asarray(w_ukv, f32)[0]),
        "sinks": np.ascontiguousarray(np.asarray(swa_sinks, f32)[0][perm].reshape(1, 8)),
        "w_out": np.ascontiguousarray(np.asarray(w_out, f32)[0]),
        "fgain": np.ascontiguousarray(np.asarray(final_gain, f32).reshape(1, 1024)),
        "cf": cf,
        "cbf": cb,
    }
    in_maps = []
    for i in range(NCORES):
        cs = c[i * NB:(i + 1) * NB]
        cT = np.ascontiguousarray(cs.reshape(NB, 8, 128).transpose(2, 1, 0).reshape(128, 32))
        m = dict(shared)
        m["x"] = x[i * NB:(i + 1) * NB]
        m["pos"] = positions[i * NB:(i + 1) * NB]
        m["cT"] = cT
        in_maps.append(m)
    res = run_bass_kernel_spmd(nc, in_maps, core_ids=list(range(NCORES)))
    out = np.concatenate([np.asarray(r["out"], f32) for r in res.results], axis=0)
    return out
```

```python
import numpy as np
from contextlib import ExitStack
import concourse.bass as bass
import concourse.mybir as mybir
from concourse.bass_utils import run_bass_kernel_spmd

F32 = mybir.dt.float32
BF16 = mybir.dt.bfloat16
I32 = mybir.dt.int32
ALU = mybir.AluOpType
AF = mybir.ActivationFunctionType

NCORES = 8
NB = 4
S = 2048
D = 1024
CH = 512
NCH = S // CH
EPS = 1e-6
NEG = -30000.0
SC_MLA = float(96 ** -0.5)
SC_SWA = 0.125
TWO_PI = float(2.0 * np.pi)

C_ZQ, C_ZKV, C_K1, C_K2, C_GM, C_QS, C_VS, C_GS = 0, 384, 640, 768, 832, 1344, 1856, 1984
NWIN = 2496
SLOT_OF_G = {0: 0, 2: 1, 1: 2, 3: 3}


class Op:
    __slots__ = ("eng", "fn", "deps", "needs_sig", "sigval", "dma_slot", "dma_cnt", "dma_wait", "is_out")

    def __init__(self, eng, fn):
        self.eng = eng
        self.fn = fn
        self.deps = ()
        self.needs_sig = False
        self.sigval = 0
        self.dma_slot = None
        self.dma_cnt = 0
        self.is_out = False


class Prog:
    ENGS = ("pe", "act", "dve", "pool", "sp")

    def __init__(self):
        self.streams = {e: [] for e in self.ENGS}
        self.last_w = {}
        self.readers = {}
        self.slot_cnt = {}

    def op(self, eng, fn, reads=(), writes=(), dma_slot=None, is_out=False, dma_batch=1):
        o = Op(eng, fn)
        deps = {}
        for k in reads:
            w = self.last_w.get(k)
            if w is not None:
                deps[id(w)] = w
        for k in writes:
            w = self.last_w.get(k)
            if w is not None:
                deps[id(w)] = w
            for r in self.readers.get(k, ()):
                deps[id(r)] = r
        o.deps = tuple(deps.values())
        for k in reads:
            self.readers.setdefault(k, []).append(o)
        for k in writes:
            self.last_w[k] = o
            self.readers[k] = []
        if dma_slot is not None:
            o.dma_slot = dma_slot
            n = self.slot_cnt.get(dma_slot, 0) + 1
            self.slot_cnt[dma_slot] = n
            o.dma_cnt = 16 * n
            o.dma_wait = 16 * (-(-n // dma_batch)) * dma_batch
            o.is_out = is_out
        self.streams[eng].append(o)
        return o

    def finalize(self):
        for e in self.ENGS:
            for o in self.streams[e]:
                for d in o.deps:
                    if d.dma_slot is not None:
                        continue
                    if d.eng == "pe" and o.eng == "pe" and o.dma_slot is None:
                        continue
                    d.needs_sig = True
        for e in self.ENGS:
            c = 0
            for o in self.streams[e]:
                if o.needs_sig and o.dma_slot is None:
                    c += 1
                    o.sigval = c

    def emit(self, engine_handle, ename, esem, slotsem):
        waited = {}
        for o in self.streams[ename]:
            need = {}
            for d in o.deps:
                if d.dma_slot is not None:
                    key = ("d", d.dma_slot)
                    sem, val = slotsem[d.dma_slot], d.dma_wait
                else:
                    if d.eng == "pe" and ename == "pe" and o.dma_slot is None:
                        continue
                    key = ("e", d.eng)
                    sem, val = esem[d.eng], d.sigval
                if val > need.get(key, (None, 0))[1]:
                    need[key] = (sem, val)
            for key, (sem, val) in need.items():
                if waited.get(key, 0) >= val:
                    continue
                engine_handle.wait_ge(sem, val)
                waited[key] = val
            ins = o.fn(engine_handle)
            if o.dma_slot is not None:
                ins.then_inc(slotsem[o.dma_slot], 16)
            elif o.needs_sig:
                ins.then_inc(esem[ename], 1)
        if ename == "sp":
            for slot, n in self.slot_cnt.items():
                if slot[0] == "out":
                    engine_handle.wait_ge(slotsem[slot], 16 * n)


def rap(base, dims):
    p = base.ap[0]
    return bass.AP(base.tensor, base.offset, [[p[0], p[1]]] + [list(d) for d in dims])


def build_program():
    nc = bass.Bass("TRN2", target_bir_lowering=False)

    def din(name, shape, dt=F32):
        return nc.dram_tensor(name, shape, dt, kind="ExternalInput").ap()

    x_d = din("x", [NB, S, D])
    pos_d = din("pos", [NB, S], I32)
    cT_d = din("cT", [128, 32])
    wada_d = din("w_ada", [1024, 3072])
    bada_d = din("b_ada", [1, 3072])
    ngT_d = din("ngT", [128, 8])
    win_d = din("w_in", [1024, 2464])
    qgT_d = din("qgT", [128, 3])
    kvgT_d = din("kvgT", [128, 2])
    wuq_d = din("w_uq", [384, 768])
    wukv_d = din("w_ukv", [256, 1024])
    sinks_d = din("sinks", [1, 8])
    wout_d = din("w_out", [1024, 1024])
    fg_d = din("fgain", [1, 1024])
    cf_d = din("cf", [128, 40])
    cbf_d = din("cbf", [128, 896])
    out_d = nc.dram_tensor("out", [NB, S, D], F32, kind="ExternalOutput").ap()
    scr_d = nc.dram_tensor("scr", [4, 3072], F32, kind="Internal").ap()

    P = Prog()

    with ExitStack() as es:
        E = es.enter_context

        def T(name, shape, dt):
            return E(nc.sbuf_tensor("sb_" + name, shape, dt))

        win_b = T("win_b", [128, 8, NWIN], BF16)
        wuq_b = T("wuq_b", [128, 3, 1024], BF16)
        wukvK = T("wukvK", [128, 2, 512], BF16)
        wukvV = T("wukvV", [128, 2, 512], BF16)
        wout_bg = T("wout_bg", [128, 8, 1024], BF16)
        cbf = T("cbf", [128, 896], BF16)
        cf = T("cf", [128, 40], F32)
        fg_bc = T("fg_bc", [128, 1024], F32)
        small = T("small", [128, 160], F32)
        smalli = T("smalli", [128, 2], I32)
        arena = T("arena", [128, 14336], F32)
        arena_b = arena.bitcast(BF16)
        KsT = T("KsT", [128, 2, 640], BF16)
        VsS = T("VsS", [128, 5, 192], BF16)
        hT = T("hT", [128, 8, 512], BF16)
        QT = T("QT", [128, 8, 512], BF16)
        QsT = T("QsT", [128, 2 * 4 * 4 * 128], BF16)
        sg = T("sg", [128, 8, 512], BF16)
        sqzn = T("sqzn", [128, 5, 512], BF16)
        sqzn_f = sqzn.bitcast(F32)
        u_t = T("u_t", [128, 512], BF16)
        PTt = T("PTt", [128, 4, 512], BF16)
        PT = [PTt[:, i, :] for i in range(4)]
        PTf = PTt.bitcast(F32)
        rstd_q = rap(PTf[:, 0, :], [[1, 512]])
        rstd_kv = rap(PTf[:, 2, :], [[1, 512]])
        ogT = T("ogT", [128, 8, 512], BF16)
        rden = [T(f"rden{i}", [128, 512], F32) for i in range(2)]
        CS = T("CS", [128, 512], F32)
        posi = T("posi", [128, 512], I32)
        posi_f = posi.bitcast(F32)
        posf = T("posf", [128, 512], F32)
        ALB = T("ALB", [128, 512], BF16)
        xbuf = [T(f"xbuf{i}", [128, 1024], F32) for i in range(3)]
        xn = T("xn", [128, 1024], BF16)

        ps = [E(nc.psum_tensor(f"ps{i}", [128, 512], F32)) for i in range(8)]
        ps_bf = [p.bitcast(BF16) for p in ps]

        ident = cbf[:, 0:128]
        ones_b = cbf[:, 128:256]
        maskC = cbf[:, 256:384]
        maskP = cbf[:, 384:512]
        sel2 = cbf[:, 512:640]
        sinkL = [cbf[:, 640:768], cbf[:, 768:896]]

        def kT(h, a, b):
            return arena_b[:, h * 2048 + a:h * 2048 + b]

        VOFF = 16384

        def vst(blk, h):
            o = VOFF + blk * 768 + (h // 2) * 192 + (h % 2) * 64
            return arena_b[:, o:o + 128]

        stage = [arena[:, i * 3072:(i + 1) * 3072] for i in range(3)]
        modtok = arena[:, 9216:12288]

        A_all = small[:, 0:32]
        Sh_all = small[:, 32:64]
        ngT = small[:, 64:72]
        qgT = small[:, 72:75]
        nqgT = small[:, 75:78]
        kvgT = small[:, 78:80]
        esink = small[:, 80:88]
        cactT = small[:, 96:128]
        ssA = small[:, 128:132]
        ssF = small[:, 132:136]
        pos0f = small[:, 136:137]
        eskv = small[:, 140:148]
        eskf = small[:, 148:156]
        gate_bc = rap(sqzn_f[:, 0, :], [[1, 1024]])

        bank_ctr = [0]

        def nb(pool=8):
            i = bank_ctr[0] % pool
            bank_ctr[0] += 1
            return i

        def BK(i):
            return ("ps", i)

        ALLSTG = ("stgall",)

        P.op("pool", lambda e: e.dma_start(out=cf[:], in_=cf_d), writes=["cf"], dma_slot=("ld", "cf"))
        P.op("pool", lambda e: e.dma_start(out=rden[0][:, 0:448], in_=cbf_d[:, 0:448]),
             writes=[("rden", 0)], dma_slot=("ld", "cb0"))
        P.op("pool", lambda e: e.dma_start(out=rden[1][:, 0:448], in_=cbf_d[:, 448:896]),
             writes=[("rden", 1)], dma_slot=("ld", "cb1"))
        P.op("dve", lambda e: e.tensor_copy(out=cbf[:, 0:448], in_=rden[0][:, 0:448]),
             reads=[("rden", 0)], writes=["cbf"])
        P.op("dve", lambda e: e.tensor_copy(out=cbf[:, 448:896], in_=rden[1][:, 0:384 + 64]),
             reads=[("rden", 1)], writes=["cbf"])
        P.op("sp", lambda e: e.dma_start(out=cactT, in_=cT_d), writes=["cact"], dma_slot=("ld", "s0"))
        P.op("pool", lambda e: e.dma_start(out=ngT, in_=ngT_d), writes=["ngT"], dma_slot=("ld", "s1"))
        P.op("pool", lambda e: e.dma_start(out=qgT, in_=qgT_d), writes=["qgT"], dma_slot=("ld", "s2"))
        P.op("pool", lambda e: e.dma_start(out=kvgT, in_=kvgT_d), writes=["kvgT"], dma_slot=("ld", "s3"))
        P.op("pool", lambda e: e.dma_start(out=esink, in_=sinks_d.partition_broadcast(128)),
             writes=["esink"], dma_slot=("ld", "s4"))
        P.op("pool", lambda e: e.dma_start(out=fg_bc[:], in_=fg_d.partition_broadcast(128)),
             writes=["fg"], dma_slot=("ld", "s5"))
        P.op("pool", lambda e: e.dma_start(out=modtok[0:4, :], in_=bada_d.partition_broadcast(4)),
             writes=["modtok"], dma_slot=("ld", "s6"))
        P.op("act", lambda e: e.activation(out=cactT, in_=cactT, func=AF.Silu), reads=[], writes=["cact"])
        P.op("act", lambda e: e.activation(out=esink, in_=esink, func=AF.Exp), writes=["esink"])
        P.op("dve", lambda e: e.tensor_scalar(out=nqgT, in0=qgT, scalar1=-1.0, scalar2=None, op0=ALU.mult),
             reads=["qgT"], writes=["nqgT"])

        P.op("pool", lambda e: e.memset(u_t[0:64, :], 0.0), writes=["eskhl"])
        P.op("pool", lambda e: e.memset(KsT[64:128, :, :], 0.0), writes=[("KsTa", 0), ("KsTa", 1)])
        P.op("pool", lambda e: e.memset(QsT[64:128, :], 0.0), writes=[("QsTa", h) for h in range(8)])
        for kv in range(2):
            r = slice(kv * 32, kv * 32 + 2)
            P.op("dve", (lambda e, r=r: e.tensor_copy(out=u_t[r, 0:8], in_=esink[r, :])),
                 reads=["esink"], writes=["eskhl"])
            P.op("dve", (lambda e, r=r: e.tensor_copy(out=eskf[r, :], in_=u_t[r, 0:8])),
                 reads=["eskhl"], writes=["eskf"])
            P.op("dve", (lambda e, r=r: e.scalar_tensor_tensor(out=eskv[r, :], in0=eskf[r, :], scalar=cf[r, 33:34],
                                                              in1=esink[r, :], op0=ALU.mult, op1=ALU.add)),
                 reads=["eskf", "esink", "cf"], writes=["eskv"])
            P.op("dve", (lambda e, r=r, kv=kv: e.tensor_copy(out=rap(u_t[r, 0:512], [[128, 4], [1, 128]]),
                                                             in_=eskv[r, kv * 4:kv * 4 + 4].to_broadcast([2, 4, 128]))),
                 reads=["eskv"], writes=["eskhl"])

        for kc in range(8):
            st = stage[kc % 3]
            P.op("sp", (lambda e, st=st, kc=kc: e.dma_start(out=st, in_=wada_d[kc * 128:(kc + 1) * 128, :])),
                 writes=[("stg", kc % 3)], dma_slot=("ld", "wada", kc % 3))
            for j in range(6):
                P.op("pe", (lambda e, st=st, kc=kc, j=j: e.matmul(
                    ps[j][0:4, :], lhsT=cactT[:, kc * 4:(kc + 1) * 4], rhs=st[:, j * 512:(j + 1) * 512],
                    start=(kc == 0), stop=(kc == 7))),
                    reads=["cact", ("stg", kc % 3), ALLSTG], writes=[BK(j)])
        for j in range(6):
            P.op("dve", (lambda e, j=j: e.tensor_tensor(
                out=modtok[0:4, j * 512:(j + 1) * 512], in0=ps[j][0:4, :],
                in1=modtok[0:4, j * 512:(j + 1) * 512], op=ALU.add)),
                reads=[ALLSTG], writes=[BK(j), "modtok"])
        P.op("sp", lambda e: e.dma_start(out=scr_d, in_=modtok[0:4, :]), reads=["modtok", ALLSTG],
             writes=["scr"], dma_slot=("ld", "scr"))
        for i in range(16):
            P.op("pe", (lambda e, i=i: e.transpose(out=ps[6][:, i * 4:(i + 1) * 4],
                                                   in_=modtok[0:4, i * 128:(i + 1) * 128],
                                                   identity=cf[0:4, 29:33])),
                 reads=["modtok", "cf", ALLSTG], writes=[BK(6)])
        P.op("dve", lambda e: e.tensor_copy(out=rap(Sh_all, [[8, 4], [1, 8]]),
                                            in_=rap(ps[6][:, 0:32], [[1, 4], [4, 8]])),
             writes=[BK(6), "Sh"])
        P.op("dve", lambda e: e.tensor_scalar(out=rap(A_all, [[8, 4], [1, 8]]),
                                              in0=rap(ps[6][:, 32:64], [[1, 4], [4, 8]]),
                                              scalar1=1.0, scalar2=None, op0=ALU.add),
             writes=[BK(6), "A"])
        P.op("dve", lambda e: e.tensor_tensor(out=rap(A_all, [[8, 4], [1, 8]]), in0=rap(A_all, [[8, 4], [1, 8]]),
                                              in1=rap(ngT, [[0, 4], [1, 8]]), op=ALU.mult),
             reads=["ngT"], writes=["A"])

        cast_engs = ["dve", "act"]
        ce = [0]

        def cast(out_ap, in_ap, reads, writes, neg=False):
            eng = cast_engs[ce[0] % 2]
            ce[0] += 1
            if neg:
                eng = "dve"
                P.op(eng, lambda e: e.tensor_scalar(out=out_ap, in0=in_ap, scalar1=-1.0, scalar2=None, op0=ALU.mult),
                     reads=reads, writes=writes)
            elif eng == "act":
                P.op(eng, lambda e: e.activation(out=out_ap, in_=in_ap, func=AF.Copy), reads=reads, writes=writes)
            else:
                P.op(eng, lambda e: e.tensor_copy(out=out_ap, in_=in_ap), reads=reads, writes=writes)

        for kc in range(8):
            si = kc % 3
            st = stage[si]
            P.op("sp", (lambda e, st=st, kc=kc: e.dma_start(out=st[:, 0:2464], in_=win_d[kc * 128:(kc + 1) * 128, :])),
                 writes=[("stg", si)], dma_slot=("ld", "stg", si))
            rk = [("stg", si), ALLSTG]
            wk = [("win", kc)]
            cast(win_b[:, kc, 0:640], st[:, 0:640], rk, wk)
            cast(win_b[:, kc, 640:704], st[:, 1696:1760], rk, wk)
            cast(win_b[:, kc, 704:736], st[:, 640:672], rk, wk)
            cast(win_b[:, kc, 736:752], st[:, 656:672], rk, wk, neg=True)
            cast(win_b[:, kc, 752:768], st[:, 640:656], rk, wk)
            cast(win_b[:, kc, 768:832], st[:, 1760:1824], rk, wk)
            cast(win_b[:, kc, 832:1856], st[:, 672:1696], rk, wk)
            cast(win_b[:, kc, 1856:2496], st[:, 1824:2464], rk, wk)

        P.op("sp", lambda e: e.dma_start(out=rap(stage[2][:, 0:2304], [[768, 3], [1, 768]]),
                                         in_=wuq_d.rearrange("(k p) n -> p k n", p=128)),
             writes=[("stg", 2)], dma_slot=("ld", "stg", 2))
        for kc in range(3):
            src = stage[2][:, kc * 768:(kc + 1) * 768]
            dst = wuq_b[:, kc, :]
            rk = [("stg", 2), ALLSTG, "qgT", "nqgT"]
            P.op("dve", (lambda e, src=src, dst=dst, kc=kc: e.tensor_scalar(
                out=rap(dst[:, 0:96], [[128, 8], [1, 96]]), in0=rap(src[:, 0:96], [[96, 8], [1, 96]]),
                scalar1=qgT[:, kc:kc + 1], scalar2=None, op0=ALU.mult)), reads=rk, writes=["wuq"])
            P.op("dve", (lambda e, src=src, dst=dst, kc=kc: e.tensor_scalar(
                out=rap(dst[:, 96:112], [[128, 8], [1, 16]]), in0=rap(src[:, 80:96], [[96, 8], [1, 16]]),
                scalar1=nqgT[:, kc:kc + 1], scalar2=None, op0=ALU.mult)), reads=rk, writes=["wuq"])
            P.op("dve", (lambda e, src=src, dst=dst, kc=kc: e.tensor_scalar(
                out=rap(dst[:, 112:128], [[128, 8], [1, 16]]), in0=rap(src[:, 64:80], [[96, 8], [1, 16]]),
                scalar1=qgT[:, kc:kc + 1], scalar2=None, op0=ALU.mult)), reads=rk, writes=["wuq"])
        P.op("sp", lambda e: e.dma_start(out=rap(stage[0][:, 0:2048], [[1024, 2], [1, 1024]]),
                                         in_=wukv_d.rearrange("(k p) n -> p k n", p=128)),
             writes=[("stg", 0)], dma_slot=("ld", "stg", 0))
        for kc in range(2):
            src = stage[0][:, kc * 1024:(kc + 1) * 1024]
            rk = [("stg", 0), ALLSTG, "kvgT"]
            P.op("dve", (lambda e, src=src, kc=kc: e.tensor_scalar(
                out=rap(wukvK[:, kc, 0:64], [[64, 8], [1, 64]]), in0=rap(src[:, 0:64], [[128, 8], [1, 64]]),
                scalar1=kvgT[:, kc:kc + 1], scalar2=None, op0=ALU.mult)), reads=rk, writes=["wukv"])
            P.op("dve", (lambda e, src=src, kc=kc: e.tensor_scalar(
                out=rap(wukvV[:, kc, 0:64], [[64, 8], [1, 64]]), in0=rap(src[:, 64:128], [[128, 8], [1, 64]]),
                scalar1=kvgT[:, kc:kc + 1], scalar2=None, op0=ALU.mult)), reads=rk, writes=["wukv"])

        P.op("dve", lambda e: e.memset(ALB[:, 0:1], 0.0), writes=[ALLSTG, "ALB"])
        P.op("pool", lambda e: e.memset(rap(arena_b[:, VOFF + 64:VOFF + 128], [[192, 64], [1, 64]]), 1.0),
             reads=[ALLSTG], writes=["vones"])
        P.op("pool", lambda e: e.memset(rap(VsS[:, 0, 64:128], [[192, 5], [1, 64]]), 1.0), writes=["vsones"])

        pt_ctr = [0]
        rd_ctr = [0]
        abank_ctr = [0]

        def prep_pos_dma(b):
            P.op("pool", (lambda e: e.dma_start(out=smalli[:, 0:1], in_=pos_d[b:b + 1, 0:1].partition_broadcast(128))),
                 writes=["pos0i"], dma_slot=("ld", "pos0"))

        def prep_pos_cvt(b):
            P.op("dve", lambda e: e.tensor_copy(out=pos0f, in_=smalli[:, 0:1]), reads=["pos0i"], writes=["pos0f"])

        def prep_pos(b):
            prep_pos_dma(b)
            prep_pos_cvt(b)

        xctr = [0]

        def next_xbuf():
            i = xctr[0] % 3
            xctr[0] += 1
            return i

        gate_og = rap(ogT.bitcast(F32)[:, 0, :], [[1, 1024]])
        og_keys = [("og", i, q) for i in range(4) for q in range(4)]
        qstage = [rap(QT.bitcast(F32)[:, 0, :], [[1, 1024]]), rap(QT.bitcast(F32)[:, 4, :], [[1, 1024]])]
        qst_keys = [[("QT", i) for i in range(4)] + [("QTlo", i) for i in range(4)], [("QT", i) for i in range(4, 8)] + [("QTlo", i) for i in range(4, 8)]]

        def wout_gate_dma(b):
            P.op("sp", (lambda e: e.dma_start(out=gate_og, in_=scr_d[b:b + 1, 2048:3072].partition_broadcast(128))),
                 reads=["scr"], writes=og_keys, dma_slot=("ld", "gate"))

        def wout_load(kc):
            P.op("sp", (lambda e: e.dma_start(out=qstage[kc % 2], in_=wout_d[kc * 128:(kc + 1) * 128, :])),
                 writes=qst_keys[kc % 2], dma_slot=("ld", "qst", kc % 2))

        def wout_mul(kc):
            P.op("dve", (lambda e: e.tensor_tensor(out=wout_bg[:, kc, :], in0=qstage[kc % 2], in1=gate_og, op=ALU.mult)),
                 reads=qst_keys[kc % 2] + og_keys, writes=[("wout", kc)])

        def prep_wout(b):
            wout_gate_dma(b)
            wout_load(0)
            wout_load(1)
            for kc in range(8):
                wout_mul(kc)
                if kc + 2 < 8:
                    wout_load(kc + 2)

        def P1_dma(b, c):
            t0 = c * CH
            P.op("pool", (lambda e: e.dma_start(out=posi[:], in_=pos_d[b:b + 1, t0:t0 + CH].partition_broadcast(128))),
                 writes=["posi"], dma_slot=("ld", "posi"))

        def P1_cvt(b, c):
            P.op("dve", lambda e: e.tensor_copy(out=posf[:, :], in_=posi[:, :]), reads=["posi"], writes=["posf"])

        def P1_a(b, c):
            P1_dma(b, c)
            P1_cvt(b, c)

        def CSR(par):
            return slice(64, 128) if par == 0 else slice(0, 64)

        def P1_b(b, c, par=0):
            R = CSR(par)
            P.op("dve", lambda e: e.tensor_scalar(out=CS[R, :], in0=posf[R, :], scalar1=cf[R, 0:1],
                                                  scalar2=cf[R, 1:2], op0=ALU.mult, op1=ALU.add),
                 reads=["posf", "cf"], writes=[("CS", par)])
            P.op("dve", lambda e: e.tensor_copy(out=posi[R, :], in_=CS[R, :]), reads=[("CS", par)], writes=["posi"])
            P.op("dve", lambda e: e.tensor_copy(out=posi_f[R, :], in_=posi[R, :]), writes=["posi"])

        def P1_sub(b, c, par=0):
            R = CSR(par)
            P.op("dve", lambda e: e.tensor_tensor(out=CS[R, :], in0=CS[R, :], in1=posi_f[R, :], op=ALU.subtract),
                 reads=["posi"], writes=[("CS", par)])

        def P1_sin(b, c, par=0):
            R = CSR(par)
            P.op("act", lambda e: e.activation(out=CS[R, :], in_=CS[R, :], func=AF.Sin, scale=TWO_PI),
                 writes=[("CS", par)])

        def P1_c(b, c, par=0):
            P1_sub(b, c, par)
            P1_sin(b, c, par)

        def P1_d(b, c):
            R = slice(0, 36)
            P.op("dve", lambda e: e.tensor_scalar(out=posf[R, :], in0=posf[R, :], scalar1=pos0f[R, :],
                                                  scalar2=None, op0=ALU.subtract),
                 reads=["pos0f"], writes=["posf"])
            P.op("dve", lambda e: e.tensor_scalar(out=posi_f[R, :], in0=posf[R, :], scalar1=-63.5,
                                                  scalar2=1.0 / 128.0, op0=ALU.add, op1=ALU.mult),
                 reads=["posf"], writes=["posi"])
            P.op("dve", lambda e: e.tensor_copy(out=posi[R, :], in_=posi_f[R, :]), writes=["posi"])
            P.op("dve", lambda e: e.tensor_copy(out=posi_f[R, :], in_=posi[R, :]), writes=["posi"])
            P.op("dve", lambda e: e.scalar_tensor_tensor(out=posf[R, :], in0=posi_f[R, :], scalar=-128.0,
                                                         in1=posf[R, :], op0=ALU.mult, op1=ALU.add),
                 reads=["posi"], writes=["posf"])

        def P2_a(b, c):
            R = slice(0, 36)
            P.op("dve", lambda e: e.tensor_scalar(out=ALB[R, :], in0=posf[R, :], scalar1=cf[R, 35:36],
                                                  scalar2=cf[R, 37:38], op0=ALU.mult, op1=ALU.add),
                 reads=["posf", "cf"], writes=["ALB"])
            P.op("dve", lambda e: e.scalar_tensor_tensor(out=ALB[R, :], in0=posi_f[R, :], scalar=cf[R, 36:37],
                                                         in1=ALB[R, :], op0=ALU.mult, op1=ALU.add),
                 reads=["posi", "cf"], writes=["ALB"])

        def P2_b(b, c):
            for kv in range(2):
                P.op("pool", (lambda e, kv=kv: e.dma_start(out=KsT[64:68, kv, 128:640], in_=ALB[32:36, :])),
                     reads=["ALB"], writes=[("KsTa", kv)], dma_slot=("ld", "alb"), dma_batch=10)
            for h in range(8):
                kv, g = h // 4, h % 4
                slot = SLOT_OF_G[g]
                o = (kv * 16 + slot) * 128
                P.op("pool", (lambda e, h=h, o=o: e.dma_start(out=rap(QsT[64:68, o:o + 128], [[512, 4], [1, 128]]),
                                                            in_=rap(ALB[h * 4:h * 4 + 4, 0:128], [[128, 4], [1, 128]]))),
                     reads=["ALB"], writes=[("QsTa", h)] + (["albdone"] if h == 7 else []), dma_slot=("ld", "alb"), dma_batch=10)

        a_xb = {}

        def phase_A_load(b, c, tb):
            xi = next_xbuf()
            a_xb[(b, c, tb)] = xi
            xb = xbuf[xi]
            r0 = c * CH + tb * 128
            P.op("sp", (lambda e: e.dma_start(out=xb[:], in_=x_d[b, r0:r0 + 128, :])),
                 writes=[("xbuf", xi)], dma_slot=("ld", "x", xi))

        a_bk = {}

        def A_stt(b, c, tb):
            xi = a_xb[(b, c, tb)]
            xb = xbuf[xi]
            bk = 4 + (abank_ctr[0] % 2)
            abank_ctr[0] += 1
            a_bk[(b, c, tb)] = bk
            P.op("dve", (lambda e: e.scalar_tensor_tensor(out=rap(small[:, 157:158], [[0, 1024]]), in0=xb[:], scalar=1.0, in1=xb[:],
                                                          op0=ALU.mult, op1=ALU.mult, accum_out=ssA[:, tb:tb + 1])),
                 reads=[("xbuf", xi)], writes=["junk1", ("ssA", tb)])

        def A_lnexp(b, c, tb):
            P.op("act", (lambda e: e.activation(out=ssA[:, tb:tb + 1], in_=ssA[:, tb:tb + 1], func=AF.Ln,
                                                bias=cf[:, 34:35], scale=1.0 / D)),
                 reads=["cf"], writes=[("ssA", tb)])
            P.op("act", (lambda e: e.activation(out=ssA[:, tb:tb + 1], in_=ssA[:, tb:tb + 1], func=AF.Exp, scale=-0.5)),
                 writes=[("ssA", tb)])

        def A_scale(b, c, tb):
            xi = a_xb[(b, c, tb)]
            xb = xbuf[xi]
            P.op("dve", (lambda e: e.tensor_scalar(out=xn[:], in0=xb[:], scalar1=ssA[:, tb:tb + 1], scalar2=None, op0=ALU.mult)),
                 reads=[("xbuf", xi), ("ssA", tb)], writes=["xn"])

        def A_tr(b, c, tb):
            bk = a_bk[(b, c, tb)]
            for kc in range(8):
                P.op("pe", (lambda e, kc=kc: e.transpose(out=ps_bf[bk][:, kc * 128:(kc + 1) * 128],
                                                         in_=xn[:, kc * 128:(kc + 1) * 128], identity=ident)),
                     reads=["xn", "cbf"], writes=[BK(bk)])

        def A_evac(b, c, tb):
            bk = a_bk[(b, c, tb)]
            for kc in range(8):
                P.op("dve", (lambda e, kc=kc: e.tensor_scalar(
                    out=hT[:, kc, tb * 128:(tb + 1) * 128], in0=ps_bf[bk][:, kc * 128:(kc + 1) * 128],
                    scalar1=A_all[:, b * 8 + kc:b * 8 + kc + 1], scalar2=Sh_all[:, b * 8 + kc:b * 8 + kc + 1],
                    op0=ALU.mult, op1=ALU.add)),
                    reads=["A", "Sh"], writes=[BK(bk), ("hT", kc)])

        BBANKS = [0, 1, 2, 3, 6, 7]
        bb_ctr = [0]

        def nbB():
            i = BBANKS[bb_ctr[0] % 6]
            bb_ctr[0] += 1
            return i

        def phase_B(b, c, hooks, par=0):
            RC = CSR(par)
            t0 = c * CH
            tick_i = [0]
            bb_ctr[0] = 0

            def tick():
                for f in hooks.pop(tick_i[0], ()):
                    f()
                tick_i[0] += 1

            for f in hooks.pop(-1, ()):
                f()
            if c > 0:
                P.op("pool", lambda e: e.tensor_copy(out=KsT[0:68, :, 0:128], in_=KsT[0:68, :, 512:640]),
                     reads=["albdone", ("KsT", 4), ("KsTb", 1), ("KsTa", 0), ("KsTa", 1)], writes=[("KsT", 0)])
                P.op("pool", lambda e: e.tensor_copy(out=VsS[:, 0, :], in_=VsS[:, 4, :]),
                     reads=[("VsS", 4)], writes=[("VsS", 0)])

            def proj(c0, m, bank):
                for kc in range(8):
                    P.op("pe", (lambda e, kc=kc: e.matmul(ps[bank][0:m, :], lhsT=win_b[:, kc, c0:c0 + m], rhs=hT[:, kc, :],
                                                          start=(kc == 0), stop=(kc == 7))),
                         reads=[("win", kc), ("hT", kc)], writes=[BK(bank)])

            zb = []
            for zi in range(5):
                bk = nbB()
                zb.append(bk)
                proj(C_ZQ + 128 * zi, 128, bk)
                P.op("act", (lambda e, zi=zi, bk=bk: e.activation(out=sqzn[:, zi, :], in_=ps[bk][:, :], func=AF.Square)),
                     writes=[BK(bk), ("sqzn", zi)])
                tick()
            bq, bkv = 4, 5
            for i in range(3):
                P.op("pe", (lambda e, i=i: e.matmul(ps[bq][:, :], lhsT=ones_b, rhs=sqzn[:, i, :], start=(i == 0), stop=(i == 2))),
                     reads=[("sqzn", i), "cbf"], writes=[BK(bq)])
            for i in range(2):
                P.op("pe", (lambda e, i=i: e.matmul(ps[bkv][:, :], lhsT=ones_b, rhs=sqzn[:, 3 + i, :], start=(i == 0), stop=(i == 1))),
                     reads=[("sqzn", 3 + i), "cbf"], writes=[BK(bkv)])
            kq = [("PT", 0), ("PT", 1)]
            kkv = [("PT", 2), ("PT", 3)]
            P.op("act", lambda e: e.activation(out=rstd_q, in_=ps[bq][:, :], func=AF.Ln, bias=cf[:, 34:35], scale=1.0 / 384.0),
                 reads=["cf"], writes=[BK(bq)] + kq)
            P.op("act", lambda e: e.activation(out=rstd_q, in_=rstd_q, func=AF.Exp, scale=-0.5), writes=kq)
            P.op("act", lambda e: e.activation(out=rstd_kv, in_=ps[bkv][:, :], func=AF.Ln, bias=cf[:, 34:35], scale=1.0 / 256.0),
                 reads=["cf"], writes=[BK(bkv)] + kkv)
            P.op("act", lambda e: e.activation(out=rstd_kv, in_=rstd_kv, func=AF.Exp, scale=-0.5), writes=kkv)
            for zi in range(5):
                rs_t, rk = (rstd_q, kq) if zi < 3 else (rstd_kv, kkv)
                P.op("dve", (lambda e, zi=zi, rs_t=rs_t, zbk=zb[zi]: e.tensor_tensor(out=sqzn[:, zi, :], in0=ps[zbk][:, :], in1=rs_t,
                                                                                     op=ALU.mult)),
                     reads=rk, writes=[BK(zb[zi]), ("sqzn", zi)])
            tick()

            bk = nbB()
            proj(C_K1, 128, bk)
            P.op("dve", (lambda e, bk=bk: e.tensor_copy(out=KsT[0:64, 0, 128:640], in_=ps[bk][0:64, :])),
                 writes=[BK(bk), ("KsT", 1), ("KsT", 2), ("KsT", 3), ("KsT", 4)])
            P.op("dve", (lambda e, bk=bk: e.tensor_tensor(out=u_t[64:128, :], in0=ps[bk][64:128, :], in1=CS[RC, :], op=ALU.mult)),
                 reads=[("CS", par)], writes=[BK(bk), "u"])
            tick()
            bk = nbB()
            proj(C_K2, 64, bk)
            P.op("dve", (lambda e, bk=bk: e.tensor_copy(out=KsT[0:64, 1, 128:640], in_=ps[bk][0:64, :])),
                 writes=[BK(bk), ("KsTb", 1)])
            tick()
            for i in range(8):
                bk = nbB()
                proj((C_GM if i < 4 else C_GS) + 128 * (i % 4), 128, bk)
                P.op("act", (lambda e, i=i, bk=bk: e.activation(out=sg[:, i, :], in_=ps[bk][:, :], func=AF.Silu)),
                     writes=[BK(bk), ("sg", i)])
                tick()
            for j in range(4):
                bk = nbB()
                proj(C_QS + 128 * j, 128, bk)
                for half in range(2):
                    h = 2 * j + half
                    kv, g = h // 4, h % 4
                    slot = SLOT_OF_G[g]
                    o = (kv * 16 + slot) * 128
                    P.op("dve", (lambda e, bk=bk, half=half, o=o: e.tensor_copy(
                        out=rap(QsT[0:64, o:o + 128], [[512, 4], [1, 128]]),
                        in_=rap(ps[bk][half * 64:half * 64 + 64, 0:128], [[128, 4], [1, 128]]))),
                        writes=[BK(bk), ("QsT", h)])
                tick()
            bk = nbB()
            for tb in range(4):
                for kc in range(8):
                    P.op("pe", (lambda e, kc=kc, tb=tb, bk=bk: e.matmul(
                        ps[bk][:, tb * 128:(tb + 1) * 128], lhsT=hT[:, kc, tb * 128:(tb + 1) * 128],
                        rhs=win_b[:, kc, C_VS:C_VS + 128], start=(kc == 0), stop=(kc == 7))),
                        reads=[("win", kc), ("hT", kc)], writes=[BK(bk)])
            P.op("dve", (lambda e, bk=bk: e.tensor_copy(out=rap(VsS[:, 1, 0:64], [[192, 4], [1, 64]]),
                                                        in_=rap(ps[bk][:, 0:64], [[128, 4], [1, 64]]))),
                 reads=["vsones"], writes=[BK(bk)] + [("VsS", s) for s in range(1, 5)])
            P.op("dve", (lambda e, bk=bk: e.tensor_copy(out=rap(VsS[:, 1, 128:192], [[192, 4], [1, 64]]),
                                                        in_=rap(ps[bk][:, 64:128], [[128, 4], [1, 64]]))),
                 writes=[BK(bk)] + [("VsS", s) for s in range(1, 5)])
            tick()

            for h in range(8):
                bk = nbB()
                for kc in range(3):
                    P.op("pe", (lambda e, h=h, kc=kc, bk=bk: e.matmul(
                        ps[bk][:, :], lhsT=wuq_b[:, kc, h * 128:(h + 1) * 128], rhs=sqzn[:, kc, :],
                        start=(kc == 0), stop=(kc == 2))),
                        reads=["wuq", ("sqzn", kc)], writes=[BK(bk)])
                P.op("act", (lambda e, h=h, bk=bk: e.activation(out=QT[0:64, h, :], in_=ps[bk][0:64, :], func=AF.Copy)),
                     reads=[BK(bk)], writes=[("QTlo", h)])
                P.op("dve", (lambda e, h=h, bk=bk: e.tensor_tensor(out=QT[64:128, h, :], in0=ps[bk][64:128, :], in1=CS[RC, :], op=ALU.mult)),
                     reads=[BK(bk), ("CS", par)], writes=[("QT", h)])
                tick()
            for h in range(8):
                bk = nbB()
                P.op("pe", (lambda e, bk=bk: e.matmul(ps[bk][:, :], lhsT=sel2, rhs=u_t[:, :], start=True, stop=True)),
                     reads=["u", "cbf", "eskhl"], writes=[BK(bk)])
                for kc in range(2):
                    P.op("pe", (lambda e, h=h, kc=kc, bk=bk: e.matmul(
                        ps[bk][0:64, :], lhsT=wukvK[:, kc, h * 64:(h + 1) * 64], rhs=sqzn[:, 3 + kc, :],
                        start=(kc == 0), stop=(kc == 1))),
                        reads=["wukv", ("sqzn", 3 + kc)], writes=[BK(bk)])
                P.op("act", (lambda e, h=h, bk=bk: e.activation(out=kT(h, t0, t0 + CH), in_=ps[bk][:, :], func=AF.Copy)),
                     reads=[ALLSTG], writes=[BK(bk), ("kT", h, c)])
                tick()
            for tb in range(4):
                bk = nbB()
                blk = c * 4 + tb
                for kc in range(2):
                    P.op("pe", (lambda e, kc=kc, tb=tb, bk=bk: e.matmul(
                        ps[bk][:, :], lhsT=sqzn[:, 3 + kc, tb * 128:(tb + 1) * 128], rhs=wukvV[:, kc, :],
                        start=(kc == 0), stop=(kc == 1))),
                        reads=["wukv", ("sqzn", 3 + kc)], writes=[BK(bk)])
                ob = VOFF + blk * 768
                P.op("dve", (lambda e, bk=bk, ob=ob: e.tensor_copy(
                    out=rap(arena_b[:, ob:ob + 64], [[192, 4], [1, 64]]), in_=rap(ps[bk][:, 0:64], [[128, 4], [1, 64]]))),
                    reads=[ALLSTG, "vones"], writes=[BK(bk), ("vst", blk)])
                P.op("dve", (lambda e, bk=bk, ob=ob: e.tensor_copy(
                    out=rap(arena_b[:, ob + 128:ob + 192], [[192, 4], [1, 64]]), in_=rap(ps[bk][:, 64:128], [[128, 4], [1, 64]]))),
                    reads=[ALLSTG], writes=[BK(bk), ("vst", blk)])
                tick()
            for k in sorted(hooks):
                for f in hooks[k]:
                    f()

        LOOK = 3
        OBANKS = [3, 6, 7]

        def attention(b, c, side):
            inject = {}
            nsteps = 8 * (4 * c + 4) + LOOK
            for k, f in enumerate(side):
                if f is not None:
                    inject.setdefault((k * nsteps) // max(len(side), 1), []).append(f)
            steps = []
            for h in range(8):
                nj = 4 * c + 4
                for j in range(nj):
                    steps.append((h, j, nj))
            info = {}

            def emit_qk(si):
                h, j, nj = steps[si]
                bk = nb(3)
                pt = pt_ctr[0] % 4
                pt_ctr[0] += 1
                i = j - 4 * c
                c0 = 128 * i if i >= 0 else 0
                diag = i >= 0
                P.op("pe", (lambda e: e.matmul(ps[bk][:, c0:512], lhsT=kT(h, j * 128, (j + 1) * 128), rhs=QT[:, h, c0:512],
                                               start=True, stop=(not diag))),
                     reads=[("kT", h, j // 4), ("QT", h), ("QTlo", h)], writes=[BK(bk)])
                if diag:
                    P.op("pe", (lambda e: e.matmul(ps[bk][:, c0:c0 + 128], lhsT=ident, rhs=maskC, start=False, stop=True)),
                         reads=["cbf"], writes=[BK(bk)])
                P.op("act", (lambda e: e.activation(out=PT[pt][:, c0:512], in_=ps[bk][:, c0:512], func=AF.Exp, scale=SC_MLA)),
                     writes=[BK(bk), ("PT", pt)])
                info[si] = (pt, c0)

            def emit_pv(si):
                h, j, nj = steps[si]
                pt, c0 = info[si]
                ob = OBANKS[h % 3]
                P.op("pe", (lambda e: e.matmul(ps[ob][:, c0:512], lhsT=vst(j, h), rhs=PT[pt][:, c0:512],
                                               start=(j == 0), stop=(j == nj - 1))),
                     reads=[("PT", pt), ("vst", j), "vones"], writes=[BK(ob)])
                if j == nj - 1:
                    rd = rd_ctr[0] % 2
                    rd_ctr[0] += 1
                    m = h // 2
                    if h % 2 == 0:
                        o_, den_, dst = slice(0, 64), slice(64, 128), slice(0, 64)
                    else:
                        o_, den_, dst = slice(64, 128), slice(0, 64), slice(64, 128)
                    if c == 3 or (c == 2 and h % 4 != 0) or (c == 1 and h % 4 == 3):
                        P.op("dve", (lambda e: e.reciprocal(out=ps[ob][den_, :], in_=ps[ob][den_, :])),
                             reads=[BK(ob)], writes=[("pden", ob)])
                    else:
                        P.op("act", (lambda e: e.activation(out=ps[ob][den_, :], in_=ps[ob][den_, :], func=AF.Ln)),
                             reads=[BK(ob)], writes=[("pden", ob)])
                        P.op("act", (lambda e: e.activation(out=ps[ob][den_, :], in_=ps[ob][den_, :], func=AF.Exp, scale=-1.0)),
                             reads=[BK(ob)], writes=[("pden", ob)])
                    P.op("dve", (lambda e: e.tensor_tensor(out=rden[rd][dst, :], in0=ps[ob][o_, :], in1=sg[dst, m, :], op=ALU.mult)),
                         reads=[BK(ob), ("sg", m)], writes=[("rden", rd)])
                    P.op("dve", (lambda e: e.tensor_tensor(out=ogT[dst, m, :], in0=ps[ob][den_, :], in1=rden[rd][dst, :], op=ALU.mult)),
                         reads=[BK(ob), ("pden", ob), ("rden", rd)], writes=[("og", m, q) for q in range(4)])

            sw = []
            for qb in range(4):
                for kv in range(2):
                    blk = 4 * c + qb
                    parts = ([] if blk == 0 else [(qb, True)]) + [(qb + 1, False)]
                    for pi, (slot, isprev) in enumerate(parts):
                        sw.append((kv, qb, slot, isprev, pi == 0, pi == len(parts) - 1))
            info2 = {}

            def sw_qk(si):
                kv, qb, slot, isprev, first, last = sw[si]
                bk = nb(3)
                pt = pt_ctr[0] % 4
                pt_ctr[0] += 1
                qo = (kv * 4 + qb) * 512
                P.op("pe", (lambda e: e.matmul(ps[bk][:, :], lhsT=KsT[:, kv, slot * 128:(slot + 1) * 128],
                                               rhs=QsT[:, qo:qo + 512], start=True, stop=False)),
                     reads=["albdone", ("KsT", slot), ("KsTa", kv), ("KsTb", 1)] + [("QsT", kv * 4 + g) for g in range(4)]
                     + [("QsTa", kv * 4 + g) for g in range(4)], writes=[BK(bk)])
                mk = maskP if isprev else maskC
                for g in range(4):
                    P.op("pe", (lambda e, g=g: e.matmul(ps[bk][:, g * 128:(g + 1) * 128], lhsT=ident, rhs=mk, start=False, stop=(g == 3))),
                         reads=["cbf"], writes=[BK(bk)])
                P.op("act", (lambda e: e.activation(out=PT[pt][:, :], in_=ps[bk][:, :], func=AF.Exp, scale=SC_SWA)),
                     writes=[BK(bk), ("PT", pt)])
                info2[si] = pt

            def sw_pv(si):
                kv, qb, slot, isprev, first, last = sw[si]
                pt = info2[si]
                ob = OBANKS[(qb * 2 + kv + 2) % 3]
                P.op("pe", (lambda e: e.matmul(ps[ob][:, :], lhsT=VsS[:, slot, kv * 64:kv * 64 + 128], rhs=PT[pt][:, :],
                                               start=first, stop=False)),
                     reads=[("PT", pt), ("VsS", slot), "vsones"], writes=[BK(ob)])
                if last:
                    P.op("pe", (lambda e: e.matmul(ps[ob][:, :], lhsT=sinkL[kv], rhs=u_t[:, :], start=False, stop=True)),
                         reads=["eskhl", "cbf", "u"], writes=[BK(ob)])
                    for par in range(2):
                        rd = rd_ctr[0] % 2
                        rd_ctr[0] += 1
                        cs_ = slice(par * 256, par * 256 + 256)
                        dst = slice(0, 64) if par == 0 else slice(64, 128)
                        if kv == 0:
                            o_, den_ = slice(0, 64), slice(64, 128)
                        else:
                            o_, den_ = slice(64, 128), slice(0, 64)
                        m0 = 4 + 2 * kv
                        if par == 0:
                            P.op("act", (lambda e, den_=den_: e.activation(out=ps[ob][den_, :], in_=ps[ob][den_, :], func=AF.Ln)),
                                 reads=[BK(ob)], writes=[("pden", ob)])
                            P.op("act", (lambda e, den_=den_: e.activation(out=ps[ob][den_, :], in_=ps[ob][den_, :], func=AF.Exp, scale=-1.0)),
                                 reads=[BK(ob)], writes=[("pden", ob)])
                        P.op("dve", (lambda e, rd=rd, cs_=cs_, dst=dst, o_=o_: e.tensor_tensor(
                            out=rap(rden[rd][dst, cs_], [[128, 2], [1, 128]]), in0=rap(ps[ob][o_, cs_], [[128, 2], [1, 128]]),
                            in1=rap(sg[dst, m0, qb * 128:(qb + 1) * 128], [[512, 2], [1, 128]]), op=ALU.mult)),
                            reads=[BK(ob), ("sg", m0), ("sg", m0 + 1)], writes=[("rden", rd)])
                        P.op("dve", (lambda e, rd=rd, cs_=cs_, dst=dst, den_=den_: e.tensor_tensor(
                            out=rap(ogT[dst, m0, qb * 128:(qb + 1) * 128], [[512, 2], [1, 128]]),
                            in0=rap(ps[ob][den_, cs_], [[128, 2], [1, 128]]), in1=rap(rden[rd][dst, cs_], [[128, 2], [1, 128]]),
                            op=ALU.mult)),
                            reads=[BK(ob), ("pden", ob), ("rden", rd)], writes=[("og", m0, qb), ("og", m0 + 1, qb)])

            items = [("m", i) for i in range(len(steps))] + [("s", i) for i in range(len(sw))]
            for k in range(len(items) + LOOK):
                if k < len(items):
                    kind, i = items[k]
                    (emit_qk if kind == "m" else sw_qk)(i)
                if k - LOOK >= 0:
                    kind, i = items[k - LOOK]
                    (emit_pv if kind == "m" else sw_pv)(i)
                for f in inject.pop(k, ()):
                    f()
            for k in sorted(inject):
                for f in inject[k]:
                    f()

        f_xb = {}

        def phase_F_load(b, c, tb):
            xi = next_xbuf()
            f_xb[(b, c, tb)] = xi
            xb = xbuf[xi]
            r0 = c * CH + tb * 128
            P.op("sp", (lambda e: e.dma_start(out=xb[:], in_=x_d[b, r0:r0 + 128, :])),
                 writes=[("xbuf", xi)], dma_slot=("ld", "x", xi))

        sgx = rap(sg.bitcast(F32)[:, 0, :], [[1, 1024]])
        sgx_keys = [("sg", i) for i in range(4)]

        def phase_F(b, c):
            t0 = c * CH
            for tb in range(3):
                if (b, c, tb) not in f_xb:
                    phase_F_load(b, c, tb)
            r3 = t0 + 3 * 128
            P.op("sp", (lambda e: e.dma_start(out=sgx, in_=x_d[b, r3:r3 + 128, :])),
                 writes=sgx_keys, dma_slot=("ld", "sgx"))
            for tb in range(4):
                if tb < 3:
                    xi = f_xb[(b, c, tb)]
                    xb = xbuf[xi][:, :]
                    xk = [("xbuf", xi)]
                    oslot = ("out", xi)
                else:
                    xb = sgx
                    xk = sgx_keys
                    oslot = ("out", 3)
                r0 = t0 + tb * 128
                fbk = [(0, 1), (2, 3), (6, 7), (4, 5)][tb]
                for mlo in (0, 4):
                    for nh in range(2):
                        bk = fbk[nh]
                        for m in range(mlo, mlo + 4):
                            P.op("pe", (lambda e, m=m, nh=nh, tb=tb, bk=bk: e.matmul(
                                ps[bk][:, :], lhsT=ogT[:, m, tb * 128:(tb + 1) * 128], rhs=wout_bg[:, m, nh * 512:(nh + 1) * 512],
                                start=(m == 0), stop=(m == 7))),
                                reads=[("og", m, tb), ("wout", m)], writes=[BK(bk)])
                for nh in range(2):
                    bk = fbk[nh]
                    P.op("dve", (lambda e, xb=xb, nh=nh, bk=bk: e.tensor_tensor(
                        out=xb[:, nh * 512:(nh + 1) * 512], in0=ps[bk][:, :], in1=xb[:, nh * 512:(nh + 1) * 512], op=ALU.add)),
                        writes=[BK(bk)] + xk)
                P.op("act", (lambda e, xb=xb, tb=tb: e.activation(out=xn[:], in_=xb, func=AF.Square, accum_out=ssF[:, tb:tb + 1])),
                     reads=xk, writes=["xn", ("ssF", tb)])
                P.op("act", (lambda e, tb=tb: e.activation(out=ssF[:, tb:tb + 1], in_=ssF[:, tb:tb + 1], func=AF.Ln,
                                                           bias=cf[:, 34:35], scale=1.0 / D)),
                     reads=["cf"], writes=[("ssF", tb)])
                P.op("act", (lambda e, tb=tb: e.activation(out=ssF[:, tb:tb + 1], in_=ssF[:, tb:tb + 1], func=AF.Exp, scale=-0.5)),
                     writes=[("ssF", tb)])
                P.op("dve", (lambda e, xb=xb, tb=tb: e.scalar_tensor_tensor(out=xb, in0=xb, scalar=ssF[:, tb:tb + 1], in1=fg_bc[:],
                                                                            op0=ALU.mult, op1=ALU.mult)),
                     reads=[("ssF", tb), "fg"], writes=xk)
                P.op("sp", (lambda e, xb=xb, r0=r0: e.dma_start(out=out_d[b, r0:r0 + 128, :], in_=xb)),
                     reads=xk, dma_slot=oslot, is_out=True)

        chunks = [(b, c) for b in range(NB) for c in range(NCH)]
        prep_pos(0)
        P1_a(0, 0); P1_b(0, 0, 0); P1_c(0, 0, 0); P1_d(0, 0)
        for tb in range(3):
            phase_A_load(0, 0, tb)
        for tb in range(3):
            A_stt(0, 0, tb)
        for tb in range(3):
            A_lnexp(0, 0, tb)
        A_scale(0, 0, 0); A_tr(0, 0, 0); phase_A_load(0, 0, 3); A_evac(0, 0, 0)
        A_scale(0, 0, 1); A_tr(0, 0, 1); A_stt(0, 0, 3); A_lnexp(0, 0, 3); A_evac(0, 0, 1)
        A_scale(0, 0, 2); A_tr(0, 0, 2); A_evac(0, 0, 2)
        A_scale(0, 0, 3); A_tr(0, 0, 3); A_evac(0, 0, 3)
        for idx, (b, c) in enumerate(chunks):
            hooks = {}

            def H(k, f):
                hooks.setdefault(k, []).append(f)

            par = idx % 2
            H(0, lambda b=b, c=c: P2_a(b, c))
            H(1, lambda b=b, c=c: P2_b(b, c))
            if idx + 1 < len(chunks):
                n1 = chunks[idx + 1]
                if n1[1] == 0:
                    H(-1, lambda n1=n1: prep_pos_dma(n1[0]))
                    H(6, lambda n1=n1: prep_pos_cvt(n1[0]))
                H(2, lambda n1=n1: P1_dma(*n1))
                H(6, lambda n1=n1: P1_cvt(*n1))
                H(8, lambda n1=n1, par=par: P1_b(n1[0], n1[1], 1 - par))
                H(9, lambda n1=n1, par=par: P1_sub(n1[0], n1[1], 1 - par))
                H(11, lambda n1=n1: P1_d(*n1))
                H(23, lambda n1=n1, par=par: P1_sin(n1[0], n1[1], 1 - par))
            if c == 0:
                H(-1, lambda b=b: (wout_gate_dma(b), wout_load(0), wout_load(1)))
                for kc in range(8):
                    H(5 + 2 * kc, lambda kc=kc: (wout_mul(kc), wout_load(kc + 2) if kc + 2 < 8 else None))
            if idx + 1 < len(chunks):
                n_ = chunks[idx + 1]
                H(-1, lambda n_=n_: [phase_A_load(n_[0], n_[1], t) for t in range(3)])
                sched = {9: [("stt", 0)], 11: [("stt", 1)], 16: [("ln", 0), ("ln", 1)], 17: [("sc", 0)],
                         20: [("tr", 0), ("ld", 3)], 21: [("sc", 1)], 24: [("tr", 1), ("fld", 0)], 26: [("ev", 0)], 31: [("ev", 1)]}
                fmap = {"stt": A_stt, "ln": A_lnexp, "sc": A_scale, "tr": A_tr, "ev": A_evac}
                for k, items in sched.items():
                    for kind, t in items:
                        if kind == "ld":
                            H(k, lambda n_=n_, t=t: phase_A_load(n_[0], n_[1], t))
                        elif kind == "fld":
                            H(k, lambda b=b, c=c, t=t: phase_F_load(b, c, t))
                        else:
                            H(k, lambda n_=n_, t=t, fn=fmap[kind]: fn(n_[0], n_[1], t))
            else:
                H(-1, lambda b=b, c=c: [phase_F_load(b, c, t) for t in range(3)])
            phase_B(b, c, hooks, par)
            NSLOT = 9
            slots = [[] for _ in range(NSLOT)]
            if idx + 1 < len(chunks):
                n_ = chunks[idx + 1]
                A2 = [(0, [("stt", 2)]), (1, [("ln", 2)]), (2, [("sc", 2)]), (3, [("tr", 2), ("stt", 3), ("fld", 1)]),
                      (4, [("ev", 2), ("ln", 3)]), (5, [("sc", 3)]), (6, [("tr", 3), ("fld", 2)]), (7, [("ev", 3)])]
                for k, items in A2:
                    for kind, t in items:
                        if kind == "fld":
                            slots[k].append(lambda b=b, c=c, t=t: phase_F_load(b, c, t))
                        else:
                            slots[k].append(lambda n_=n_, t=t, fn=fmap[kind]: fn(n_[0], n_[1], t))
            side = [(lambda fs=fs: [f() for f in fs]) if fs else None for fs in slots]
            attention(b, c, side)
            phase_F(b, c)

        P.finalize()
        esem = {e: E(nc.semaphore(f"s_{e}")) for e in Prog.ENGS}
        slotsem = {}
        for i, slot in enumerate(P.slot_cnt.keys()):
            slotsem[slot] = E(nc.semaphore(f"d_{i}"))
        block = E(nc.Block())

        @block.tensor
        def _(e):
            P.emit(e, "pe", esem, slotsem)

        @block.scalar
        def _(e):
            P.emit(e, "act", esem, slotsem)

        @block.vector
        def _(e):
            P.emit(e, "dve", esem, slotsem)

        @block.gpsimd
        def _(e):
            P.emit(e, "pool", esem, slotsem)

        @block.sync
        def _(e):
            P.emit(e, "sp", esem, slotsem)

    return nc


def _consts():
    cf = np.zeros((128, 40), np.float32)
    p = np.arange(128)
    inv = (10000.0 ** (-np.arange(0, 32, 2, dtype=np.float32) / 32.0)).astype(np.float32)
    cf[:, 0] = (inv[p % 16].astype(np.float64) / (2.0 * np.pi)).astype(np.float32)
    cf[:, 1] = np.where((p % 64) < 32, 0.25, 0.0)
    cf[64, 2] = 1.0
    cf[66, 3] = 1.0
    cf[65, 4] = 1.0
    cf[67, 4] = 1.0
    for h in range(8):
        sl = 2.0 ** (-(h + 1))
        cf[65, 5 + h] = -8.0 * sl
        cf[67, 13 + h] = -1024.0 * sl
        cf[64, 21 + h] = 8.0 * sl
        cf[66, 21 + h] = 1024.0 * sl
    cf[0:4, 29:33] = np.eye(4, dtype=np.float32)
    for h in range(8):
        sl = 2.0 ** (-(h + 1))
        cf[h * 4 + 0, 37] = 8.0 * sl
        cf[h * 4 + 1, 35] = -8.0 * sl
        cf[h * 4 + 2, 37] = 1024.0 * sl
        cf[h * 4 + 3, 36] = -1024.0 * sl
    cf[32, 35] = 1.0
    cf[33, 37] = 1.0
    cf[34, 36] = 1.0
    cf[35, 37] = 1.0
    cf[0, 33] = 0.0
    cf[1, 33] = -1.0
    cf[32, 33] = 0.0
    cf[33, 33] = -1.0
    cf[:, 34] = EPS
    cb = np.zeros((128, 896), np.float32)
    cb[:, 0:128] = np.eye(128)
    cb[:, 128:256] = 1.0
    s_ = np.arange(128)[:, None]
    t_ = np.arange(128)[None, :]
    cb[:, 256:384] = np.where(s_ <= t_, 0.0, NEG)
    cb[:, 384:512] = np.where(s_ > t_, 0.0, NEG)
    for k in range(64):
        cb[64 + k, 512 + 64 + (k % 32)] = 1.0
        cb[64 + k, 512 + 96 + (k % 32)] = 1.0
    cb[0:2, 640 + 64:640 + 128] = 1.0
    cb[32:34, 768:768 + 64] = 1.0
    return cf, cb


_NC_CACHE = {}


def kernel(x, c, positions, w_ada, b_ada, norm_gain, w_in, q_norm_gain, kv_norm_gain,
           w_uq, w_ukv, swa_sinks, w_out, final_gain):
    f32 = np.float32
    x = np.ascontiguousarray(np.asarray(x, f32))
    c = np.asarray(c, f32)
    positions = np.ascontiguousarray(np.asarray(positions, np.int32))
    if "nc" not in _NC_CACHE:
        _NC_CACHE["nc"] = build_program()
    nc = _NC_CACHE["nc"]
    cf, cb = _consts()
    perm = [0, 2, 1, 3, 4, 6, 5, 7]
    shared = {
        "w_ada": np.ascontiguousarray(np.asarray(w_ada, f32)[0]),
        "b_ada": np.ascontiguousarray(np.asarray(b_ada, f32)[0].reshape(1, 3072)),
        "ngT": np.ascontiguousarray(np.asarray(norm_gain, f32)[0].reshape(8, 128).T),
        "w_in": np.ascontiguousarray(np.asarray(w_in, f32)[0]),
        "qgT": np.ascontiguousarray(np.asarray(q_norm_gain, f32)[0].reshape(3, 128).T),
        "kvgT": np.ascontiguousarray(np.asarray(kv_norm_gain, f32)[0].reshape(2, 128).T),
        "w_uq": np.ascontiguousarray(np.asarray(w_uq, f32)[0]),
        "w_ukv": np.ascontiguousarray(np.asarray(w_ukv, f32)[0]),
        "sinks": np.ascontiguousarray(np.asarray(swa_sinks, f32)[0][perm].reshape(1, 8)),
        "w_out": np.ascontiguousarray(np.asarray(w_out, f32)[0]),
        "fgain": np.ascontiguousarray(np.asarray(final_gain, f32).reshape(1, 1024)),
        "cf": cf,
        "cbf": cb,
    }
    in_maps = []
    for i in range(NCORES):
        cs = c[i * NB:(i + 1) * NB]
        cT = np.ascontiguousarray(cs.reshape(NB, 8, 128).transpose(2, 1, 0).reshape(128, 32))
        m = dict(shared)
        m["x"] = x[i * NB:(i + 1) * NB]
        m["pos"] = positions[i * NB:(i + 1) * NB]
        m["cT"] = cT
        in_maps.append(m)
    res = run_bass_kernel_spmd(nc, in_maps, core_ids=list(range(NCORES)))
    out = np.concatenate([np.asarray(r["out"], f32) for r in res.results], axis=0)
    return out
```
